# Optimizing a Trainium2 kernel written in Bass

```python
import jax, jax.numpy as jnp
from jax import lax
import numpy as np

D_MODEL = 1024
BATCH = 8
SEQ = 2048
DEPTH = 1

MEM_LEN = 256
D_MIX = D_MODEL
D_SGU = D_MIX // 2
SGU_HEADS = 4
SGU_HEAD_DIM = D_SGU // SGU_HEADS
CHUNK = 128
D_RWKV = D_MIX - D_SGU
RWKV_HEAD_DIM = 64
RWKV_HEADS = D_RWKV // RWKV_HEAD_DIM
DECAY_LORA = 64
ICL_LORA = 64
C_SGU = 3 * D_SGU
C_RWKV = 4 * D_RWKV + DECAY_LORA + ICL_LORA
C_IN = C_SGU + C_RWKV
XATTN_HEADS = 4
XATTN_HEAD_DIM = D_MODEL // XATTN_HEADS
RMS_EPS = 1e-6
LN_EPS = 1e-5
GN_EPS = 64e-5

kernel_name = 'hybrid_sgu_rwkv7_memxattn'


def rms_norm(x, g):
    xf = x.astype(jnp.float32)
    y = xf * lax.rsqrt(jnp.mean(xf * xf, axis=-1, keepdims=True) + RMS_EPS)
    return (y * g.astype(jnp.float32)).astype(x.dtype)


def layer_norm(x, g, b):
    xf = x.astype(jnp.float32)
    mu = jnp.mean(xf, axis=-1, keepdims=True)
    var = jnp.mean(jnp.square(xf - mu), axis=-1, keepdims=True)
    y = (xf - mu) * lax.rsqrt(var + LN_EPS)
    return (y * g.astype(jnp.float32) + b.astype(jnp.float32)).astype(x.dtype)


def token_shift(z):
    return jnp.pad(z, ((0, 0), (1, 0), (0, 0)))[:, :-1]


def sgu_group(z, ln_g, ln_b, ws, bs, out_g):
    b, s, _ = z.shape
    u, v, gate = jnp.split(z, 3, axis=-1)
    u = jax.nn.gelu(u, approximate=False)
    v = layer_norm(jax.nn.gelu(v, approximate=False), ln_g, ln_b)
    v = v.reshape(b, s // CHUNK, CHUNK, SGU_HEADS, SGU_HEAD_DIM)
    causal = jnp.tril(jnp.ones((CHUNK, CHUNK), dtype=bool))
    ws_c = jnp.where(causal[None], ws, jnp.zeros_like(ws))
    sv = jnp.einsum('hts,bcshd->bcthd', ws_c, v) + jnp.swapaxes(bs, 0, 1)[None, None, :, :, None]
    y = rms_norm(u * sv.reshape(b, s, D_SGU), out_g)
    return y * jax.nn.silu(gate)


def rwkv7_step(state, inp):
    r, w, k, v, a, bb = inp
    sa = jnp.einsum('bhvk,bhk->bhv', state, a)
    state = state * w[:, :, None, :] + sa[..., None] * bb[:, :, None, :] + v[..., None] * k[:, :, None, :]
    y = jnp.einsum('bhvk,bhk->bhv', state, r)
    return state, y


def rwkv7_group(z, mu, w0, w2, a0, a2, k_k, k_a, r_k, gn_g, gn_b):
    b, s, _ = z.shape
    z = z + (token_shift(z) - z) * mu
    r, k, v, gate, wd, ad = jnp.split(
        z, [D_RWKV, 2 * D_RWKV, 3 * D_RWKV, 4 * D_RWKV, 4 * D_RWKV + DECAY_LORA], axis=-1)
    logw = -jax.nn.softplus(-(w0 + jnp.tanh(wd) @ w2)) - 0.5
    decay = jnp.exp(-jnp.exp(logw.astype(jnp.float32)))
    icl = jax.nn.sigmoid(a0 + ad @ a2)
    heads = lambda t: t.astype(jnp.float32).reshape(b, s, RWKV_HEADS, RWKV_HEAD_DIM)
    kk = heads(k * k_k)
    kk = kk * lax.rsqrt(jnp.maximum(jnp.sum(kk * kk, axis=-1, keepdims=True), 1e-24))
    k = k * (1 + (icl - 1) * k_a)
    rh, wh, kh, vh, ah = heads(r), heads(decay), heads(k), heads(v), heads(icl)
    seq_major = lambda t: jnp.moveaxis(t, 1, 0)
    state0 = jnp.zeros((b, RWKV_HEADS, RWKV_HEAD_DIM, RWKV_HEAD_DIM), jnp.float32)
    _, y = lax.scan(rwkv7_step, state0,
                    (seq_major(rh), seq_major(wh), seq_major(kh), seq_major(vh),
                     seq_major(-kk), seq_major(kk * ah)))
    y = jnp.moveaxis(y, 0, 1)
    mean = jnp.mean(y, axis=-1, keepdims=True)
    var = jnp.mean(jnp.square(y - mean), axis=-1, keepdims=True)
    yn = ((y - mean) * lax.rsqrt(var + GN_EPS)).reshape(b, s, D_RWKV)
    yn = yn * gn_g.astype(jnp.float32) + gn_b.astype(jnp.float32)
    bonus = jnp.sum(rh * kh * r_k.astype(jnp.float32), axis=-1, keepdims=True) * vh
    out = (yn + bonus.reshape(b, s, D_RWKV)).astype(z.dtype)
    return out * jax.nn.silu(gate)


def mem_cross_attention(h, mem, g_x, g_mem, w_q, w_kv, w_o):
    b, s, _ = h.shape
    m = mem.shape[1]
    hn = rms_norm(h, g_x)
    mn = rms_norm(mem, g_mem)
    q = (hn @ w_q).reshape(b, s, XATTN_HEADS, XATTN_HEAD_DIM)
    k, v = jnp.split(mn @ w_kv, 2, axis=-1)
    k = k.reshape(b, m, XATTN_HEADS, XATTN_HEAD_DIM)
    v = v.reshape(b, m, XATTN_HEADS, XATTN_HEAD_DIM)
    scores = jnp.einsum('bqhd,bkhd->bhqk', q, k).astype(jnp.float32) * (XATTN_HEAD_DIM ** -0.5)
    p = jax.nn.softmax(scores, axis=-1).astype(v.dtype)
    o = jnp.einsum('bhqk,bkhd->bqhd', p, v).reshape(b, s, D_MODEL)
    return o @ w_o


def setup_inputs(seed: int = 0) -> dict:
    key = jax.random.key(seed)
    ks = jax.random.split(key, 26)
    L = DEPTH
    nrm = lambda k, shape, scale: scale * jax.random.normal(k, shape, jnp.float32)
    gain = lambda k, shape: 1.0 + 0.01 * jax.random.normal(k, shape, jnp.float32)
    return {
        'x': nrm(ks[0], (BATCH, SEQ, D_MODEL), 1.0),
        'mem': nrm(ks[1], (BATCH, MEM_LEN, D_MODEL), 1.0),
        'ln_mix_g': gain(ks[2], (L, D_MODEL)),
        'w_in': nrm(ks[3], (L, D_MODEL, C_IN), D_MODEL ** -0.5),
        'sgu_ln_g': gain(ks[4], (L, D_SGU)),
        'sgu_ln_b': nrm(ks[5], (L, D_SGU), 0.01),
        'sgu_ws': nrm(ks[6], (L, SGU_HEADS, CHUNK, CHUNK), CHUNK ** -0.5),
        'sgu_bs': gain(ks[7], (L, SGU_HEADS, CHUNK)),
        'sgu_out_g': gain(ks[8], (L, D_SGU)),
        'rw_mu': jax.random.uniform(ks[9], (L, C_RWKV), jnp.float32),
        'rw_w0': jax.random.uniform(ks[10], (L, D_RWKV), jnp.float32, -4.0, 0.0),
        'rw_w2': nrm(ks[11], (L, DECAY_LORA, D_RWKV), 0.1),
        'rw_a0': nrm(ks[12], (L, D_RWKV), 0.1),
        'rw_a2': nrm(ks[13], (L, ICL_LORA, D_RWKV), 0.1),
        'rw_k_k': 0.85 + 0.01 * jax.random.normal(ks[14], (L, D_RWKV), jnp.float32),
        'rw_k_a': gain(ks[15], (L, D_RWKV)),
        'rw_r_k': nrm(ks[16], (L, RWKV_HEADS, RWKV_HEAD_DIM), 0.1),
        'rw_gn_g': gain(ks[17], (L, D_RWKV)),
        'rw_gn_b': nrm(ks[18], (L, D_RWKV), 0.01),
        'w_out': nrm(ks[19], (L, D_MIX, D_MODEL), D_MIX ** -0.5),
        'ln_x_g': gain(ks[20], (L, D_MODEL)),
        'ln_mem_g': gain(ks[21], (L, D_MODEL)),
        'w_q': nrm(ks[22], (L, D_MODEL, D_MODEL), D_MODEL ** -0.5),
        'w_kv': nrm(ks[23], (L, D_MODEL, 2 * D_MODEL), D_MODEL ** -0.5),
        'w_o': nrm(ks[24], (L, D_MODEL, D_MODEL), D_MODEL ** -0.5),
        'ln_f_g': gain(ks[25], (D_MODEL,)),
    }


def reference(x, mem, ln_mix_g, w_in, sgu_ln_g, sgu_ln_b, sgu_ws, sgu_bs, sgu_out_g,
              rw_mu, rw_w0, rw_w2, rw_a0, rw_a2, rw_k_k, rw_k_a, rw_r_k, rw_gn_g, rw_gn_b,
              w_out, ln_x_g, ln_mem_g, w_q, w_kv, w_o, ln_f_g):
    h = x
    for l in range(DEPTH):
        z = rms_norm(h, ln_mix_g[l]) @ w_in[l]
        y_sgu = sgu_group(z[..., :C_SGU], sgu_ln_g[l], sgu_ln_b[l], sgu_ws[l], sgu_bs[l], sgu_out_g[l])
        y_rwkv = rwkv7_group(z[..., C_SGU:], rw_mu[l], rw_w0[l], rw_w2[l], rw_a0[l], rw_a2[l],
                             rw_k_k[l], rw_k_a[l], rw_r_k[l], rw_gn_g[l], rw_gn_b[l])
        h = h + jnp.concatenate([y_sgu, y_rwkv], axis=-1) @ w_out[l]
        h = h + mem_cross_attention(h, mem, ln_x_g[l], ln_mem_g[l], w_q[l], w_kv[l], w_o[l])
    return rms_norm(h, ln_f_g)
```

```python
import math
import threading
import numpy as np
from contextlib import ExitStack
import concourse.bass as bass
import concourse.mybir as mybir
from concourse.bass_utils import run_bass_kernel_spmd

F32 = mybir.dt.float32
BF16 = mybir.dt.bfloat16
AF = mybir.ActivationFunctionType
ALU = mybir.AluOpType
AX = mybir.AxisListType

S = 2048
D = 1024
NTILE = 16
CIN = 3712
C0 = math.exp(-0.5)
RMS_EPS = 1e-6
LN_EPS = 1e-5
GN_EPS = 64e-5


class Eng:
    def __init__(self, name):
        self.name = name
        self.prog = []
        self.count = 0
        self.seen = {}
        self.sem = None


class Buf:
    _n = 0

    def __init__(self, t, name=None):
        self.t = t
        self.name = name
        self.w = None
        self.r = []
        self.dsem = None
        self.dcnt = 0
        self.alias = []
        self.xd = []
        self.lo = self.hi = 0
        Buf._n += 1

    def __getitem__(self, k):
        return self.t[k]


class Rec:
    def __init__(self):
        self.call = None

    def __getattr__(self, name):
        def f(*a, **k):
            self.call = (name, a, k)
            return self
        return f


class FW:
    SYNC_LAT = 0.25
    AUX_BIAS = 0.15

    def __init__(self, nc):
        self.nc = nc
        self.E = {n: Eng(n) for n in ("sync", "scalar", "vector", "gpsimd", "tensor")}
        self.dma_bufs = []
        self.out_bufs = []
        self.hook = None
        self.arb = None
        self.atomic = False
        self.efree = {n: 0.0 for n in self.E}
        self.fin = {}
        self.dfin = {}

    def _need(self, eng, key, val):
        if eng.seen.get(key, 0) < val:
            eng.seen[key] = val
            eng.prog.append(("wait", key, val))

    def _collect(self, eng, reads, writes):
        pe = eng.name == "tensor"
        out = []
        for b in reads:
            if b.w is not None:
                e2, s = b.w
                if not (e2 is eng and pe):
                    out.append(("eng", e2.name, s))
            for a in b.alias:
                if a.w is not None and not (a.w[0] is eng and pe):
                    out.append(("eng", a.w[0].name, a.w[1]))
            if b.dcnt:
                out.append(("dma", id(b), b.dcnt))
            for t in b.xd:
                out.append(("dma", id(t), t.dcnt))
        for b in writes:
            for t in b.xd:
                out.append(("dma", id(t), t.dcnt))
            for a in b.alias:
                if a.w is not None:
                    out.append(("eng", a.w[0].name, a.w[1]))
                for (e2, s) in a.r:
                    out.append(("eng", e2.name, s))
            if b.w is not None:
                e2, s = b.w
                if not (e2 is eng and pe):
                    out.append(("eng", e2.name, s))
            for (e2, s) in b.r:
                if not (e2 is eng and pe):
                    out.append(("eng", e2.name, s))
            if b.dcnt:
                out.append(("dma", id(b), b.dcnt))
        return out

    def _deps(self, eng, reads, writes):
        for d in self._collect(eng, reads, writes):
            if d[0] == "eng":
                self._need(eng, ("eng", d[1]), d[2])
            else:
                self._need(eng, ("dma", d[1]), d[2])

    def _ready(self, ename, reads, writes):
        eng = self.E[ename]
        t = self.efree[ename]
        for d in self._collect(eng, reads, writes):
            if d[0] == "eng":
                f = self.fin.get((d[1], d[2]))
                if f is None:
                    f = self.efree[d[1]]
                f += 0.08 if d[1] == ename else self.SYNC_LAT
            else:
                f = self.dfin.get(d[1], 0.0)
            if f > t:
                t = f
        return t

    @staticmethod
    def _cost(ename, call):
        nm, a, k = call
        ap = k.get("out", a[0] if a else None)
        try:
            n = ap.free_size()
        except Exception:
            n = 128
        if ename == "tensor":
            if nm == "transpose":
                return 0.09
            r = k.get("rhs")
            try:
                n = r.free_size()
                f32 = (r.dtype == F32)
            except Exception:
                n, f32 = 128, False
            c = max(0.055, n / 2400.0 + 0.02)
            return c * 4 if f32 else c
        if ename == "vector":
            c = 0.12 + n / 960.0
            return c * 1.8 if nm == "tensor_tensor_scan" else c
        if ename == "scalar":
            return 0.2 + n / 1200.0
        if ename == "gpsimd":
            return 1.0 + n / 120.0
        return 0.05

    def op(self, ename, fn, reads=(), writes=(), inc=True):
        eng = self.E[ename]
        rec = Rec()
        fn(rec)
        if self.arb is not None:
            self.arb(("op", ename, reads, writes))
        start = self._ready(ename, reads, writes)
        self._deps(eng, reads, writes)
        if inc:
            eng.count += 1
            seq = eng.count
        else:
            seq = eng.count + 1
        eng.prog.append(("op", rec.call, inc))
        endt = start + self._cost(ename, rec.call)
        self.efree[ename] = endt if ename != "tensor" else start + max(0.055, self._cost(ename, rec.call))
        self.fin[(ename, seq)] = endt + (0.12 if ename == "tensor" else 0.0)
        for b in reads:
            b.r.append((eng, seq))
            if len(b.r) > 64:
                mx = {}
                for (e2, s) in b.r:
                    if mx.get(e2.name, (None, 0))[1] < s:
                        mx[e2.name] = (e2, s)
                b.r = list(mx.values())
        for b in writes:
            b.w = (eng, seq)
            b.r = []
        return seq

    def dma(self, ename, fn, buf, reads=(), writes=(), oneshot=False):
        eng = self.E[ename]
        rec = Rec()
        fn(rec)
        if self.arb is not None:
            self.arb(("op", ename, reads, writes))
        start = self._ready(ename, reads, writes)
        self._deps(eng, reads, writes)
        if oneshot:
            tokb = Buf(None, "tok")
            for b in list(reads) + list(writes):
                b.xd.append(tokb)
            buf = tokb
        if buf.dsem is None:
            buf.dsem = True
            self.dma_bufs.append(buf)
        buf.dcnt += 16
        eng.prog.append(("dma", rec.call, buf))
        self.efree[ename] = start + 0.1
        self.dfin[id(buf)] = start + 4.5
        for b in writes:
            b.w = None
            b.r = []

    def run_threads(self, fns):
        n = len(fns)
        if n == 1:
            fns[0]()
            return
        sems = [threading.Semaphore(0) for _ in range(n)]
        done = [False] * n
        prop = [None] * n
        main = threading.Semaphore(0)
        errs = []
        tl = threading.local()
        rr = [0]

        def choose():
            alive = [j for j in range(n) if not done[j]]
            if not alive:
                return None
            for j in alive:
                if prop[j] is None:
                    return j
            best, bt = None, None
            for kk_ in range(n):
                j = (rr[0] + 1 + kk_) % n
                if done[j]:
                    continue
                p = prop[j]
                t = 1e18 if p[0] == "wait" else self._ready(p[1], p[2], p[3])
                if bt is None or t < bt - 1e-9:
                    best, bt = j, t
            rr[0] = best
            return best

        def arb(desc):
            i = getattr(tl, "idx", None)
            if i is None or (self.atomic and desc[0] != "wait"):
                return
            prop[i] = desc
            j = choose()
            if j != i:
                sems[j].release()
                sems[i].acquire()

        def hook():
            arb(("wait",))

        def wrapper(i):
            sems[i].acquire()
            tl.idx = i
            try:
                if not errs:
                    fns[i]()
            except BaseException as ex:
                errs.append(ex)
            finally:
                done[i] = True
                j = choose()
                if j is not None:
                    sems[j].release()
                else:
                    main.release()

        ths = [threading.Thread(target=wrapper, args=(i,)) for i in range(n)]
        for t in ths:
            t.start()
        old = (self.hook, self.arb)
        self.hook, self.arb = hook, arb
        sems[0].release()
        main.acquire()
        for t in ths:
            t.join()
        self.hook, self.arb = old
        if errs:
            raise errs[0]

    def mark_out(self, buf):
        if buf not in self.out_bufs:
            self.out_bufs.append(buf)

    def build(self, stack):
        nc = self.nc
        semmap = {}
        for e in self.E.values():
            e.sem = stack.enter_context(nc.semaphore("s_" + e.name))
            semmap[("eng", e.name)] = e.sem
        for i, b in enumerate(self.dma_bufs):
            b.dsem = stack.enter_context(nc.semaphore("d%d" % i))
            semmap[("dma", id(b))] = b.dsem
        sy = self.E["sync"]
        for b in self.out_bufs:
            self._need(sy, ("dma", id(b)), b.dcnt)
        block = stack.enter_context(nc.Block())
        handles = {"sync": nc.sync, "scalar": nc.scalar, "vector": nc.vector,
                   "gpsimd": nc.gpsimd, "tensor": nc.tensor}

        def run(e):
            h = handles[e.name]
            for ent in e.prog:
                if ent[0] == "wait":
                    h.wait_ge(semmap[ent[1]], ent[2])
                elif ent[0] == "op":
                    nm, a, k = ent[1]
                    ins = getattr(h, nm)(*a, **k)
                    if ent[2]:
                        ins.then_inc(e.sem, 1)
                else:
                    nm, a, k = ent[1]
                    ins = getattr(h, nm)(*a, **k)
                    ins.then_inc(ent[2].dsem, 16)

        @block.sync
        def _(x):
            run(self.E["sync"])

        @block.scalar
        def _(x):
            run(self.E["scalar"])

        @block.vector
        def _(x):
            run(self.E["vector"])

        @block.gpsimd
        def _(x):
            run(self.E["gpsimd"])

        @block.tensor
        def _(x):
            run(self.E["tensor"])


def build_program(ntile=NTILE, stop=None):
    nc = bass.Bass("TRN2", target_bir_lowering=False)

    def din(name, shape):
        return nc.dram_tensor(name, shape, F32, kind="ExternalInput").ap()

    x_d = din("x", [S, D])
    mem_d = din("mem", [256, D])
    w_in_d = din("w_in", [D, CIN])
    w_out_d = din("w_out", [D, D])
    w_q_d = din("w_q", [D, D])
    w_kv_d = din("w_kv", [D, 2 * D])
    w_o_d = din("w_o", [D, D])
    gpp_d = din("gpp", [128, 32])
    lnf_d = din("lnf", [1, D])
    lng_d = din("lng", [1, 512])
    lnb_d = din("lnb", [1, 512])
    wsT_d = din("wsT", [4, 128, 128])
    bsT_d = din("bsT", [128, 4])
    rwpp_d = din("rwpp", [128, 48])
    w2_d = din("w2", [64, 512])
    a2_d = din("a2", [64, 512])
    out_d = nc.dram_tensor("out", [S, D], F32, kind="ExternalOutput").ap()

    st = ExitStack()
    with st:
        fw = FW(nc)

        class StopBuild(Exception):
            pass

        def ck(name):
            if stop == name:
                raise StopBuild()

        def record():

            def sb(shape, dt=F32, name=None):
                nm = "sb_" + (name or ("t%d" % Buf._n))
                return Buf(st.enter_context(nc.sbuf_tensor(nm, shape, dt)), nm)

            def psb(shape, dt=F32, name=None):
                nm = "ps_" + (name or ("p%d" % Buf._n))
                return Buf(st.enter_context(nc.psum_tensor(nm, shape, dt)), nm)

            V = lambda fn, r=(), w=(): fw.op("vector", fn, r, w)
            A = lambda fn, r=(), w=(): fw.op("scalar", fn, r, w)
            G = lambda fn, r=(), w=(): fw.op("gpsimd", fn, r, w)
            PE = lambda fn, r=(), w=(), last=True: fw.op("tensor", fn, r, w, inc=last)

            pf, pb = [], []
            for i in range(8):
                th_ = st.enter_context(nc.psum_tensor("ps_bank%d" % i, [128, 512], F32))
                f_ = Buf(th_, "pf%d" % i)
                b_ = Buf(th_[:].bitcast(BF16), "pb%d" % i)
                f_.alias = [b_]
                b_.alias = [f_]
                pf.append(f_)
                pb.append(b_)

            class Pool:
                def __init__(self, fidx, bidx):
                    self.f = fidx
                    self.b = bidx
                    self.fi = 0
                    self.bi = 0

            POOLS = {"main": Pool(list(range(8)), list(range(8))),
                     "pA": Pool([0, 1], [2]), "pB": Pool([3, 4], [5]), "post": Pool([6, 7], [6, 7])}
            tls = threading.local()

            def cur_pool():
                return POOLS[getattr(tls, "pool", "main")]

            def npf():
                p_ = cur_pool()
                b = pf[p_.f[p_.fi % len(p_.f)]]
                p_.fi += 1
                return b

            def npb():
                p_ = cur_pool()
                if p_.b is p_.f or p_.b == p_.f:
                    b = pb[p_.f[p_.fi % len(p_.f)]]
                    p_.fi += 1
                else:
                    b = pb[p_.b[p_.bi % len(p_.b)]]
                    p_.bi += 1
                return b

            def in_pool(name, fn):
                def g():
                    tls.pool = name
                    fn()
                return g

            ones512 = sb([128, 128], F32, "ones128")
            G(lambda e: e.memset(ones512[:], 1.0), w=[ones512])
            identf = sb([128, 128], F32, "identf")
            G(lambda e: e.affine_select(out=identf[:], in_=ones512[:, 0:128], pattern=[[-1, 128]],
                                        compare_op=ALU.is_equal, fill=0.0, base=0, channel_multiplier=1),
              r=[ones512], w=[identf])
            identb = sb([128, 128], BF16, "identb")
            G(lambda e: e.tensor_copy(out=identb[:], in_=identf[:]), r=[identf], w=[identb])
            ident2 = sb([128, 2, 128], BF16, "ident2")
            for k in range(2):
                G(lambda e, k=k: e.tensor_copy(out=ident2[:, k, :], in_=identf[:]), r=[identf], w=[ident2])
            mSU = sb([128, 4, 128], BF16, "mSU")
            mIU = sb([128, 4, 128], BF16, "mIU")
            mSL = sb([128, 2, 128], BF16, "mSL")
            for (mk, cmp_, cm, stp, ncp) in [(mSU, ALU.is_gt, -1, 1, 4), (mIU, ALU.is_ge, -1, 1, 4), (mSL, ALU.is_gt, 1, -1, 2)]:
                G(lambda e, mk=mk, cmp_=cmp_, cm=cm, stp=stp: e.affine_select(
                    out=mk[:, 0, :], in_=ones512[:], pattern=[[stp, 128]], compare_op=cmp_, fill=0.0, base=0,
                    channel_multiplier=cm), r=[ones512], w=[mk])
                for k in range(1, ncp):
                    G(lambda e, mk=mk, k=k: e.tensor_copy(out=mk[:, k, :], in_=mk[:, 0, :]), r=[mk], w=[mk])
            onesbd = sb([128, 128], F32, "onesbd")
            G(lambda e: e.memset(onesbd[:], 0.0), w=[onesbd])
            G(lambda e: e.memset(onesbd[0:64, 0:64], 1.0), w=[onesbd])
            G(lambda e: e.memset(onesbd[64:128, 64:128], 1.0), w=[onesbd])
            onesbd64 = sb([128, 128], F32, "onesbd64")
            G(lambda e: e.tensor_scalar(out=onesbd64[:], in0=onesbd[:], scalar1=1.0 / 64.0, scalar2=None,
                                        op0=ALU.mult), r=[onesbd], w=[onesbd64])

            ck('consts')
            gpp = sb([128, 32], F32, "gpp")
            fw.dma("sync", lambda e: e.dma_start(out=gpp[:], in_=gpp_d), gpp, writes=[gpp])
            rwpp = sb([128, 48], F32, "rwpp")
            fw.dma("sync", lambda e: e.dma_start(out=rwpp[:], in_=rwpp_d), rwpp, writes=[rwpp])
            bsT = sb([128, 4], F32, "bsT")
            fw.dma("sync", lambda e: e.dma_start(out=bsT[:], in_=bsT_d), bsT, writes=[bsT])
            lnf_b = sb([128, D], F32, "lnf_b")
            fw.dma("sync", lambda e: e.dma_start(out=lnf_b[:], in_=lnf_d.partition_broadcast(128)), lnf_b, writes=[lnf_b])
            lng_b = sb([128, 512], F32, "lng_b")
            fw.dma("sync", lambda e: e.dma_start(out=lng_b[:], in_=lng_d.partition_broadcast(128)), lng_b, writes=[lng_b])
            lnb_b = sb([128, 512], F32, "lnb_b")
            fw.dma("sync", lambda e: e.dma_start(out=lnb_b[:], in_=lnb_d.partition_broadcast(128)), lnb_b, writes=[lnb_b])
            w2b = sb([128, 512], BF16, "w2b")
            G(lambda e: e.memset(w2b[:], 0.0), w=[w2b])
            fw.dma("gpsimd", lambda e: e.dma_start(out=w2b[0:64, :], in_=w2_d), w2b, writes=[w2b])
            a2b = sb([128, 512], BF16, "a2b")
            G(lambda e: e.memset(a2b[:], 0.0), w=[a2b])
            fw.dma("gpsimd", lambda e: e.dma_start(out=a2b[64:128, :], in_=a2_d), a2b, writes=[a2b])
            wsTm = sb([128, 4, 128], BF16, "wsTm")
            nrw = sb([128, 8], F32, "nrw")
            V(lambda e: e.tensor_scalar(out=nrw[:], in0=rwpp[:, 17:25], scalar1=-1.0, scalar2=None, op0=ALU.mult),
              r=[rwpp], w=[nrw])
            MU, W0, A0, KK, KA, RK, GG, GB = 0, 17, 21, 25, 29, 33, 37, 41

            ck('params')
            stage = [sb([128, 1024], F32, "stage%d" % i) for i in range(2)]
            sti = [0]
            cast_eng = ["vector", "scalar"]

            def load_w(dst, dst_ap, src_ap, ncol, scale_ap, stages=None):
                stl = stages if stages is not None else stage
                sg = stl[sti[0] % len(stl)]
                eng = cast_eng[sti[0] % 2]
                sti[0] += 1
                fw.dma("sync", lambda e: e.dma_start(out=sg[:, 0:ncol], in_=src_ap), sg, writes=[sg])
                if eng == "scalar":
                    if scale_ap is None:
                        fw.op(eng, lambda e: e.copy(out=dst_ap, in_=sg[:, 0:ncol]), [sg], [dst])
                    else:
                        fw.op(eng, lambda e: e.mul(out=dst_ap, in_=sg[:, 0:ncol], mul=scale_ap), [sg, gpp], [dst])
                elif scale_ap is None:
                    fw.op(eng, lambda e: e.tensor_copy(out=dst_ap, in_=sg[:, 0:ncol]), [sg], [dst])
                else:
                    fw.op(eng, lambda e: e.tensor_scalar(out=dst_ap, in0=sg[:, 0:ncol], scalar1=scale_ap,
                                                         scalar2=None, op0=ALU.mult), [sg, gpp], [dst])

            sg0 = stage[0]
            fw.dma("sync", lambda e: e.dma_start(out=sg0[:, 0:512].rearrange("p (a b) -> p a b", b=128),
                                                 in_=wsT_d.rearrange("h s t -> s h t")), sg0, writes=[sg0])
            V(lambda e: e.tensor_tensor(out=wsTm[:], in0=sg0[:, 0:512].rearrange("p (a b) -> p a b", b=128),
                                        in1=mIU[:], op=ALU.mult), r=[sg0, mIU], w=[wsTm])

            w_in = sb([128, 8, CIN], BF16, "w_in")
            w_in_r = Buf(w_in.t, "w_in_r")
            for q2 in range(2):
                for c in range(8):
                    c0 = q2 * 768
                    load_w(w_in, w_in[:, c, c0:c0 + 768], w_in_d[c * 128:(c + 1) * 128, c0:c0 + 768], 768,
                           gpp[:, c:c + 1])

            def load_w_in_r():
                for (c0, nn) in [(1536, 726), (2262, 725), (2987, 725)]:
                    for c in range(8):
                        load_w(w_in_r, w_in[:, c, c0:c0 + nn], w_in_d[c * 128:(c + 1) * 128, c0:c0 + nn], nn,
                               gpp[:, c:c + 1])
                zdone[("wr",)] = True
            wkv_blk = sb([128, 8, 512], BF16, "wkv_blk")
            w_out = sb([128, 8, D], BF16, "w_out")
            w_q = sb([128, 8, D], BF16, "w_q")
            w_o = sb([128, 8, D], BF16, "w_o")

            def load_w_out():
                for c in range(8):
                    load_w(w_out, w_out[:, c, :], w_out_d[c * 128:(c + 1) * 128, :], 1024,
                           gpp[:, 24 + c:25 + c] if c < 4 else None)

            def load_w_q():
                for c in range(8):
                    load_w(w_q, w_q[:, c, :], w_q_d[c * 128:(c + 1) * 128, :], 1024, gpp[:, 8 + c:9 + c])

            def load_w_o():
                for c in range(8):
                    load_w(w_o, w_o[:, c, :], w_o_d[c * 128:(c + 1) * 128, :], 1024, None)

            ck('w_in')
            junk = sb([128, D], BF16, "junk")
            st1 = [sb([128, 4], F32, "st1_%d" % i) for i in range(8)]
            sti1 = [0]

            def nst():
                b = st1[sti1[0] % 8]
                sti1[0] += 1
                return b

            def rstd_from_sumsq(src_buf, src_ap, n, eps):
                ssq = nst()
                A(lambda e: e.activation(out=junk[:, 0:n], in_=src_ap, func=AF.Square, accum_out=ssq[:, 0:1]),
                  r=[src_buf], w=[junk, ssq])
                A(lambda e: e.activation(out=ssq[:, 1:2], in_=ssq[:, 0:1], func=AF.Ln, scale=1.0 / n, bias=eps),
                  r=[ssq], w=[ssq])
                rs = nst()
                A(lambda e: e.activation(out=rs[:, 0:1], in_=ssq[:, 1:2], func=AF.Exp, scale=-0.5), r=[ssq], w=[rs])
                return rs

            def sigmoid3(out_buf, out_ap, in_buf, in_ap, nbias=None, nbias_buf=None):
                rr = [in_buf] + ([nbias_buf] if nbias_buf is not None else [])
                if nbias is None:
                    A(lambda e: e.activation(out=out_ap, in_=in_ap, func=AF.Exp, scale=-1.0), r=rr, w=[out_buf])
                else:
                    A(lambda e: e.activation(out=out_ap, in_=in_ap, func=AF.Exp, scale=-1.0, bias=nbias),
                      r=rr, w=[out_buf])
                A(lambda e: e.activation(out=out_ap, in_=out_ap, func=AF.Ln, bias=1.0), r=[out_buf], w=[out_buf])
                A(lambda e: e.activation(out=out_ap, in_=out_ap, func=AF.Exp, scale=-1.0), r=[out_buf], w=[out_buf])

            def transpose_to(src_buf, src_aps, dst_buf, dst_ap3):
                n = len(src_aps)
                p = npb()
                for k, ap in enumerate(src_aps):
                    PE(lambda e, k=k, ap=ap: e.transpose(out=p[:, k * 128:(k + 1) * 128], in_=ap, identity=identb[:]),
                       r=[src_buf, identb], w=[p], last=(k == n - 1))
                A(lambda e: e.copy(out=dst_ap3, in_=p[:, 0:n * 128].rearrange("p (a b) -> p a b", b=128)),
                  r=[p], w=[dst_buf])

            ck('scratch')
            mnT = sb([128, 8, 256], BF16, "mnT")
            KT = sb([128, 8, 256], BF16, "KT")
            Vm = sb([128, 2, D], BF16, "Vm")
            xt_bufs = [sb([128, D], F32, "xt%d" % i) for i in range(2)]
            xs = sb([128, D], BF16, "xs")

            def mem_phase():
                for mt in range(2):
                    mtile = stage[0]
                    fw.dma("sync", lambda e, mt=mt, mtile=mtile: e.dma_start(out=mtile[:], in_=mem_d[mt * 128:(mt + 1) * 128, :]),
                           mtile, writes=[mtile])
                    rs = rstd_from_sumsq(mtile, mtile[:], D, RMS_EPS)
                    V(lambda e, mtile=mtile, rs=rs: e.tensor_scalar(out=hs[:], in0=mtile[:], scalar1=rs[:, 0:1], scalar2=None,
                                                                    op0=ALU.mult), r=[mtile, rs], w=[hs])
                    transpose_to(hs, [hs[:, c * 128:(c + 1) * 128] for c in range(8)], mnT,
                                 mnT[:, :, mt * 128:(mt + 1) * 128])
                for b4 in range(4):
                    for c in range(8):
                        load_w(wkv_blk, wkv_blk[:, c, :], w_kv_d[c * 128:(c + 1) * 128, b4 * 512:(b4 + 1) * 512], 512,
                               gpp[:, 16 + c:17 + c], stages=[stage[0], stage[1], xt_bufs[1]])
                    if b4 < 2:
                        for jj in range(4):
                            j = b4 * 4 + jj
                            p = npf()
                            for c in range(8):
                                PE(lambda e, jj=jj, c=c, p=p: e.matmul(p[:, 0:256], lhsT=wkv_blk[:, c, jj * 128:(jj + 1) * 128],
                                                                       rhs=mnT[:, c, :], start=(c == 0), stop=(c == 7)),
                                   r=[wkv_blk, mnT], w=[p], last=(c == 7))
                            A(lambda e, j=j, p=p: e.mul(out=KT[:, j, :], in_=p[:, 0:256], mul=1.0 / 16.0), r=[p], w=[KT])
                    else:
                        blk = b4 - 2
                        for mt in range(2):
                            p = npf()
                            for c in range(8):
                                PE(lambda e, mt=mt, c=c, p=p: e.matmul(p[:], lhsT=mnT[:, c, mt * 128:(mt + 1) * 128],
                                                                       rhs=wkv_blk[:, c, :], start=(c == 0), stop=(c == 7)),
                                   r=[wkv_blk, mnT], w=[p], last=(c == 7))
                            V(lambda e, mt=mt, blk=blk, p=p: e.tensor_copy(out=Vm[:, mt, blk * 512:(blk + 1) * 512], in_=p[:]),
                              r=[p], w=[Vm])


            ck('mem')
            xnT = sb([128, 8, 128], BF16, "xnT")
            yT = sb([128, 8, 128], BF16, "yT")
            bn6 = sb([128, 6], F32, "bn6")
            mv = sb([128, 2], F32, "mv")
            carry = sb([128, 20], F32, "carry")
            G(lambda e: e.memset(carry[:], 0.0), w=[carry])
            Hs = [sb([128, 64], F32, "Hs%d" % c) for c in range(4)]
            Hbd = [sb([128, 2, 64], BF16, "Hbd%d" % c) for c in range(4)]
            for c in range(4):
                G(lambda e, c=c: e.memset(Hs[c][:], 0.0), w=[Hs[c]])
                G(lambda e, c=c: e.memset(Hbd[c][:], 0.0), w=[Hbd[c]])
            mx = sb([128, 8], F32, "mx")
            rsum = sb([128, 8], F32, "rsum")

            RWORDS = 8800
            region = st.enter_context(nc.sbuf_tensor("sb_region", [128, RWORDS], F32))
            views = []

            def nwords(shape, dt):
                n = 1
                for d_ in shape[1:]:
                    n *= d_
                return n if dt == F32 else (n + 1) // 2

            def rv(name, off_w, shape, dt=F32):
                nw = nwords(shape, dt)
                assert off_w + nw <= RWORDS, (name, off_w, nw)
                ap = region[:, off_w:off_w + nw]
                if dt != F32:
                    ap = ap.bitcast(dt)
                if len(shape) == 3:
                    ap = ap.rearrange("p (a b) -> p a b", b=shape[2])
                vb_ = Buf(ap, name)
                vb_.lo, vb_.hi = off_w, off_w + nw
                for o in views:
                    if o.lo < vb_.hi and vb_.lo < o.hi:
                        o.alias.append(vb_)
                        vb_.alias.append(o)
                views.append(vb_)
                return vb_

            class Alloc:
                def __init__(self, base):
                    self.off = base

                def __call__(self, name, shape, dt=F32):
                    v_ = rv(name, self.off, shape, dt)
                    self.off += nwords(shape, dt)
                    return v_

            def alloc_set(base, si):
                al = Alloc(base)
                B = {}
                nm = lambda n: "%s_%d" % (n, si)
                z0 = al.off
                B["zt"] = al(nm("zt"), [128, 4, 130])
                B["df"] = al(nm("df"), [128, 4, 128])
                z1 = al.off
                al2 = Alloc(z0)
                B["SUm"] = al2(nm("SUm"), [128, 4, 128], BF16)
                B["IUm"] = al2(nm("IUm"), [128, 4, 128], BF16)
                B["SLm"] = al2(nm("SLm"), [128, 2, 128], BF16)
                o_ = al2.off
                pp0 = (al2(nm("PPa0"), [128, 2, 128], BF16), al2(nm("PPb0"), [128, 2, 128], BF16),
                       rv(nm("PPab0"), o_, [128, 4, 128], BF16))
                assert al2.off <= z1
                B["zm"] = al(nm("zm"), [128, 4, 128])
                sig0 = al.off
                for k in ["sg", "icl", "sgate", "kk", "sq", "tq", "rn", "kkn", "kki", "tk", "k2", "rk", "bsum"]:
                    B[k] = al(nm(k), [128, 128])
                B["sig3"] = rv(nm("sig3"), sig0, [128, 384])
                for k in ["bT", "kT", "vb", "Wsb", "Usb"]:
                    B[k] = al(nm(k), [128, 128], BF16)
                B["tok"] = al(nm("tok"), [128, 3, 128], BF16)
                o_ = al.off
                pp1 = (al(nm("PPa1"), [128, 2, 128], BF16), al(nm("PPb1"), [128, 2, 128], BF16),
                       rv(nm("PPab1"), o_, [128, 4, 128], BF16))
                B["PP"] = [pp0, pp1]
                B["XT"] = [al(nm("XT%d" % i), [128, 2, 128], BF16) for i in range(2)]
                B["gx"], B["gi"], B["gm"] = B["sq"], B["tq"], B["rn"]
                B["yt"], B["cen"], B["yn"], B["o1"], B["slg"] = B["kk"], B["kkn"], B["kki"], B["tk"], B["k2"]
                B["sqc"], B["tv"], B["rsv"] = B["sq"], B["tq"], B["rn"]
                return B, al.off

            SETS = []
            B0, end0 = alloc_set(0, 0)
            B1, end1 = alloc_set(end0, 1)
            SETS = [B0, B1]
            print("region set size", end0, "two sets", end1, "of", RWORDS)
            for si, B in enumerate(SETS):
                B["cs"] = sb([128, 130], F32, "cs%d" % si)
                B["aTp"] = sb([128, 2, 128], BF16, "aTp%d" % si)
                B["rTp"] = sb([128, 2, 128], BF16, "rTp%d" % si)
                B["Hg"] = sb([128, 64], F32, "Hg%d" % si)
                G(lambda e, B=B: e.memset(B["cs"][:], 0.0), w=[B["cs"]])
                G(lambda e, B=B: e.memset(B["aTp"][:], 0.0), w=[B["aTp"]])
                G(lambda e, B=B: e.memset(B["rTp"][:], 0.0), w=[B["rTp"]])
            al = Alloc(0)
            gu = al("gu", [128, 512])
            gv = al("gv", [128, 512])
            eu = al("eu", [128, 512])
            sil = al("sil", [128, 512])
            vh = al("vh", [128, 512])
            y0 = al("y0", [128, 512])
            vn = al("vn", [128, 512], BF16)
            ysg = al("ysg", [128, 512], BF16)
            assert al.off <= end0, (al.off, end0)
            wflat = wkv_blk.t[:].rearrange("p a b -> p (a b)")
            Ee = Buf(wflat[:, 0:1024].rearrange("p (a b) -> p a b", b=256), "Ee")
            PT = Buf(wflat[:, 1024:2048].rearrange("p (a b) -> p a b", b=128), "PT")
            oT = Buf(wflat[:, 2048:3072].rearrange("p (a b) -> p a b", b=128), "oT")
            hs = Buf(wflat[:, 3072:4096], "hs")
            for v_ in (Ee, PT, oT, hs):
                v_.alias = [wkv_blk]
                wkv_blk.alias.append(v_)
            al = Alloc(end1)
            zw = al("zw", [128, 130])
            dfw = al("dfw", [128, 128])
            lw = al("lw", [128, 128], BF16)
            print("region end", al.off, "of", RWORDS)
            ot = stage
            ot = stage
            hnT = Buf(mnT.t[:, :, 0:128], "hnT")
            qT = Buf(mnT.t[:, :, 128:256], "qT")
            hnT.alias = [mnT]
            qT.alias = [mnT]
            mnT.alias = [hnT, qT]

            print("sbuf bytes remaining:", nc.sbuf_bytes_remaining)

            RW0 = 1536

            def mix_chunks(zbuf, nchunk, col0, dfbuf, zmbuf):
                V(lambda e: e.tensor_tensor(out=dfbuf[:], in0=zbuf[:, :, 0:128], in1=zbuf[:, :, 1:129], op=ALU.subtract),
                  r=[zbuf], w=[dfbuf])
                for j in range(nchunk):
                    V(lambda e, j=j: e.scalar_tensor_tensor(out=zmbuf[:, j, :], in0=dfbuf[:, j, :],
                                                            scalar=rwpp[:, MU + col0 + j * 4:MU + col0 + j * 4 + 1],
                                                            in1=zbuf[:, j, 1:129], op0=ALU.mult, op1=ALU.add),
                      r=[dfbuf, zbuf, rwpp], w=[zmbuf])

            pending_post = None
            zdone = {}
            PZ = {}

            def wait_flags(keys):
                while fw.hook is not None and not all(zdone.get(k, False) for k in keys):
                    fw.hook()

            def zblk(blk):
                p = npf()
                for c in range(8):
                    PE(lambda e, c=c, p=p: e.matmul(p[:], lhsT=xnT[:, c, :], rhs=w_in[:, c, blk * 512:(blk + 1) * 512],
                                                    start=(c == 0), stop=(c == 7)),
                       r=[xnT, w_in], w=[p], last=(c == 7))
                return p

            def pre_body(it, flags):
                r0 = it * 128
                xt = xt_bufs[it % 2]
                ob = ot[it % 2] if it > 0 else xt_bufs[1]
                gvv, guu = ob[:, 0:512], ob[:, 512:1024]
                vnn, ysgg = xs[:, 0:512], xs[:, 512:1024]
                fw.dma("sync", lambda e: e.dma_start(out=xt[:], in_=x_d[r0:r0 + 128, :]), xt, writes=[xt])
                rs = rstd_from_sumsq(xt, xt[:], D, RMS_EPS)
                V(lambda e: e.tensor_scalar(out=xs[:], in0=xt[:], scalar1=rs[:, 0:1], scalar2=None,
                                            op0=ALU.mult), r=[xt, rs], w=[xs])
                wait_flags([("z", k) for k in flags])
                transpose_to(xs, [xs[:, c * 128:(c + 1) * 128] for c in range(8)], xnT, xnT[:])
                zdone[("xnT",)] = True
                pv = zblk(1)
                pu = zblk(0)
                fw.atomic = True
                A(lambda e: e.activation(out=gvv, in_=pv[:], func=AF.Gelu), r=[pv], w=[ob])
                fw.atomic = False
                fw.atomic = True
                A(lambda e: e.activation(out=guu, in_=pu[:], func=AF.Gelu), r=[pu], w=[ob])
                fw.atomic = False
                V(lambda e: e.bn_stats(out=bn6[:], in_=gvv), r=[ob], w=[bn6])
                V(lambda e: e.bn_aggr(out=mv[:], in_=bn6[:]), r=[bn6], w=[mv])
                t1 = nst()
                A(lambda e: e.activation(out=t1[:, 0:1], in_=mv[:, 1:2], func=AF.Ln, bias=LN_EPS), r=[mv], w=[t1])
                A(lambda e: e.activation(out=t1[:, 1:2], in_=t1[:, 0:1], func=AF.Exp, scale=-0.5), r=[t1], w=[t1])
                V(lambda e: e.tensor_scalar(out=gvv, in0=gvv, scalar1=mv[:, 0:1], scalar2=t1[:, 1:2],
                                            op0=ALU.subtract, op1=ALU.mult), r=[ob, mv, t1], w=[ob])
                V(lambda e: e.tensor_tensor(out=gvv, in0=gvv, in1=lng_b[:], op=ALU.mult), r=[ob, lng_b], w=[ob])
                V(lambda e: e.tensor_tensor(out=vnn, in0=gvv, in1=lnb_b[:], op=ALU.add), r=[ob, lnb_b], w=[xs])
                psv = npf()
                for hh in range(4):
                    PE(lambda e, hh=hh: e.matmul(psv[:, hh * 128:(hh + 1) * 128], lhsT=wsTm[:, hh, :],
                                                 rhs=xs[:, hh * 128:(hh + 1) * 128], start=True, stop=True),
                       r=[wsTm, xs], w=[psv], last=(hh == 3))
                for hh in range(4):
                    V(lambda e, hh=hh: e.scalar_tensor_tensor(out=ob[:, 512 + hh * 128:512 + (hh + 1) * 128],
                                                              in0=psv[:, hh * 128:(hh + 1) * 128],
                                                              scalar=bsT[:, hh:hh + 1],
                                                              in1=ob[:, 512 + hh * 128:512 + (hh + 1) * 128],
                                                              op0=ALU.add, op1=ALU.mult), r=[psv, bsT, ob], w=[ob])
                rs2 = rstd_from_sumsq(ob, guu, 512, RMS_EPS)
                pg = zblk(2)
                sigmoid3(ob, gvv, pg, pg[:])
                V(lambda e: e.tensor_tensor(out=gvv, in0=gvv, in1=pg[:], op=ALU.mult), r=[ob, pg], w=[ob])
                V(lambda e: e.scalar_tensor_tensor(out=ysgg, in0=guu, scalar=rs2[:, 0:1], in1=gvv,
                                                   op0=ALU.mult, op1=ALU.mult), r=[ob, rs2], w=[xs])
                if it == 0:
                    wait_flags([("wr",)])
                pw = npf()
                for c in range(8):
                    PE(lambda e, c=c: e.matmul(pw[:, 0:128], lhsT=w_in[:, c, RW0 + 2048:RW0 + 2176], rhs=xnT[:, c, :],
                                               start=(c == 0), stop=(c == 7)), r=[w_in_r, xnT], w=[pw], last=(c == 7))
                A(lambda e: e.copy(out=zw[:, 0:1], in_=carry[:, 16:17]), r=[carry], w=[zw])
                A(lambda e: e.copy(out=zw[:, 1:129], in_=pw[:, 0:128]), r=[pw], w=[zw])
                A(lambda e: e.copy(out=carry[:, 16:17], in_=zw[:, 128:129]), r=[zw], w=[carry])
                V(lambda e: e.tensor_tensor(out=dfw[:], in0=zw[:, 0:128], in1=zw[:, 1:129], op=ALU.subtract),
                  r=[zw], w=[dfw])
                V(lambda e: e.scalar_tensor_tensor(out=dfw[:], in0=dfw[:], scalar=rwpp[:, MU + 16:MU + 17],
                                                   in1=zw[:, 1:129], op0=ALU.mult, op1=ALU.add),
                  r=[dfw, zw, rwpp], w=[dfw])
                wait_flags([("lo", k) for k in flags])
                A(lambda e: e.copy(out=lw[64:128, :], in_=dfw[64:128, :]), r=[dfw], w=[lw])
                A(lambda e: e.activation(out=dfw[0:64, :], in_=dfw[0:64, :], func=AF.Exp, scale=-2.0), r=[dfw], w=[dfw])
                A(lambda e: e.activation(out=dfw[0:64, :], in_=dfw[0:64, :], func=AF.Ln, bias=1.0), r=[dfw], w=[dfw])
                A(lambda e: e.activation(out=dfw[0:64, :], in_=dfw[0:64, :], func=AF.Exp, scale=-1.0), r=[dfw], w=[dfw])
                V(lambda e: e.tensor_scalar(out=lw[0:64, :], in0=dfw[0:64, :], scalar1=2.0, scalar2=-1.0,
                                            op0=ALU.mult, op1=ALU.add), r=[dfw], w=[lw])

            def finish_pre():
                transpose_to(xs, [xs[:, 512 + c * 128:512 + (c + 1) * 128] for c in range(4)], yT, yT[:, 0:4, :])

            def pre0():
                pre_body(0, [])
                finish_pre()
            fw.run_threads([in_pool("post", pre0), load_w_in_r])
            for it in range(ntile):
                r0 = it * 128
                xt = xt_bufs[it % 2]
                zdone.clear()

                def emit_z(c):
                    pz = npf()
                    for j in range(4):
                        col = RW0 + j * 512 + c * 128
                        for cc in range(8):
                            PE(lambda e, j=j, cc=cc, col=col: e.matmul(pz[:, j * 128:(j + 1) * 128],
                                                                        lhsT=w_in[:, cc, col:col + 128], rhs=xnT[:, cc, :],
                                                                        start=(cc == 0), stop=(cc == 7)),
                               r=[w_in_r, xnT], w=[pz], last=(j == 3 and cc == 7))
                    return pz

                def pair_body(c, B, si=0, nxt=None):
                    zt, df, zm, sg, icl, sgate, kk, sq, tq, rn, kkn, kki, tk, k2, rk, bsum, gx, gi, gm, bT, kT, vb, tok, SUm, IUm, SLm, PP, XT, Wsb, Usb, yt, cen, yn, o1, slg, sqc, tv, rsv, cs, aTp, rTp, Hg, sig3 = [B[k] for k in ['zt', 'df', 'zm', 'sg', 'icl', 'sgate', 'kk', 'sq', 'tq', 'rn', 'kkn', 'kki', 'tk', 'k2', 'rk', 'bsum', 'gx', 'gi', 'gm', 'bT', 'kT', 'vb', 'tok', 'SUm', 'IUm', 'SLm', 'PP', 'XT', 'Wsb', 'Usb', 'yt', 'cen', 'yn', 'o1', 'slg', 'sqc', 'tv', 'rsv', 'cs', 'aTp', 'rTp', 'Hg', 'sig3']]
                    pz = PZ.pop(si, None)
                    if pz is None:
                        pz = emit_z(c)
                    A(lambda e, c=c: e.copy(out=zt[:, :, 0:1],
                                                   in_=carry[:, 0:16].rearrange("p (j c) -> p j c", c=4)[:, :, c:c + 1]),
                      r=[carry], w=[zt])
                    A(lambda e: e.copy(out=zt[:, :, 1:129], in_=pz[:].rearrange("p (a b) -> p a b", b=128)),
                      r=[pz], w=[zt])
                    A(lambda e, c=c: e.copy(out=carry[:, 0:16].rearrange("p (j c) -> p j c", c=4)[:, :, c:c + 1],
                                                   in_=zt[:, :, 128:129]), r=[zt], w=[carry])
                    zdone[("z", c)] = True
                    mix_chunks(zt, 4, c, df, zm)
                    r_, k_, v_, g_ = zm[:, 0, :], zm[:, 1, :], zm[:, 2, :], zm[:, 3, :]
                    pl = npf()
                    PE(lambda e, c=c: e.matmul(pl[:, 0:128], lhsT=w2b[:, c * 128:(c + 1) * 128], rhs=lw[:],
                                               start=True, stop=True), r=[w2b, lw], w=[pl], last=False)
                    PE(lambda e, c=c: e.matmul(pl[:, 128:256], lhsT=a2b[:, c * 128:(c + 1) * 128], rhs=lw[:],
                                               start=True, stop=True), r=[a2b, lw], w=[pl], last=True)
                    zdone[("lo", c)] = True
                    A(lambda e, c=c: e.activation(out=sg[:], in_=pl[:, 0:128], func=AF.Exp, scale=-1.0,
                                                  bias=nrw[:, c:c + 1]), r=[pl, nrw], w=[sg])
                    A(lambda e, c=c: e.activation(out=icl[:], in_=pl[:, 128:256], func=AF.Exp, scale=-1.0,
                                                  bias=nrw[:, 4 + c:5 + c]), r=[pl, nrw], w=[icl])
                    A(lambda e: e.activation(out=sgate[:], in_=g_, func=AF.Exp, scale=-1.0), r=[zm], w=[sgate])
                    A(lambda e: e.activation(out=sig3[:], in_=sig3[:], func=AF.Ln, bias=1.0), r=[sig3], w=[sig3])
                    A(lambda e: e.activation(out=sig3[:], in_=sig3[:], func=AF.Exp, scale=-1.0), r=[sig3], w=[sig3])
                    V(lambda e: e.tensor_tensor_scan(out=cs[:, 1:129], data0=ones512[:, 0:128], data1=sg[:], initial=0.0,
                                                     op0=ALU.mult, op1=ALU.add), r=[ones512, sg], w=[cs])
                    V(lambda e, c=c: e.tensor_scalar(out=kk[:], in0=k_, scalar1=rwpp[:, KK + c:KK + c + 1], scalar2=None,
                                                     op0=ALU.mult), r=[zm, rwpp], w=[kk])
                    A(lambda e, c=c: e.activation(out=sq[:], in_=k_, func=AF.Square, scale=rwpp[:, KK + c:KK + c + 1]), r=[zm, rwpp], w=[sq])
                    pn = npf()
                    PE(lambda e: e.matmul(pn[:, 0:128], lhsT=onesbd[:], rhs=sq[:], start=True, stop=True),
                       r=[onesbd, sq], w=[pn])
                    V(lambda e: e.tensor_scalar(out=tq[:], in0=pn[:, 0:128], scalar1=1e-24, scalar2=None, op0=ALU.max),
                      r=[pn], w=[tq])
                    A(lambda e: e.activation(out=tq[:], in_=tq[:], func=AF.Ln), r=[tq], w=[tq])
                    A(lambda e: e.activation(out=rn[:], in_=tq[:], func=AF.Exp, scale=-0.5), r=[tq], w=[rn])
                    V(lambda e: e.tensor_tensor(out=kkn[:], in0=kk[:], in1=rn[:], op=ALU.mult), r=[kk, rn], w=[kkn])
                    V(lambda e: e.tensor_tensor(out=kki[:], in0=kkn[:], in1=icl[:], op=ALU.mult), r=[kkn, icl], w=[kki])
                    V(lambda e, c=c: e.tensor_scalar(out=tk[:], in0=icl[:], scalar1=-1.0, scalar2=rwpp[:, KA + c:KA + c + 1],
                                                     op0=ALU.add, op1=ALU.mult), r=[icl, rwpp], w=[tk])
                    V(lambda e: e.scalar_tensor_tensor(out=k2[:], in0=tk[:], scalar=1.0, in1=k_, op0=ALU.add, op1=ALU.mult),
                      r=[tk, zm], w=[k2])
                    V(lambda e, c=c: e.scalar_tensor_tensor(out=rk[:], in0=r_, scalar=rwpp[:, RK + c:RK + c + 1], in1=k2[:],
                                                            op0=ALU.mult, op1=ALU.mult), r=[zm, rwpp, k2], w=[rk])
                    pbs = npf()
                    PE(lambda e: e.matmul(pbs[:, 0:128], lhsT=onesbd[:], rhs=rk[:], start=True, stop=True),
                       r=[onesbd, rk], w=[pbs])
                    V(lambda e: e.tensor_tensor(out=o1[:], in0=pbs[:, 0:128], in1=v_, op=ALU.mult), r=[pbs, zm], w=[o1])
                    A(lambda e: e.activation(out=gx[:], in_=cs[:, 0:128], func=AF.Exp, scale=-C0), r=[cs], w=[gx])
                    A(lambda e: e.activation(out=gi[:], in_=cs[:, 1:129], func=AF.Exp, scale=C0), r=[cs], w=[gi])
                    A(lambda e: e.activation(out=gm[:], in_=cs[:, 1:129], func=AF.Exp, scale=-C0), r=[cs], w=[gm])
                    for hh in range(2):
                        P = slice(hh * 64, hh * 64 + 64)
                        V(lambda e, hh=hh, P=P: e.scalar_tensor_tensor(out=aTp[P, hh, :], in0=kkn[P, :], scalar=-1.0,
                                                                       in1=gx[P, :], op0=ALU.mult, op1=ALU.mult),
                          r=[kkn, gx], w=[aTp])
                    V(lambda e: e.tensor_tensor(out=bT[:], in0=kki[:], in1=gi[:], op=ALU.mult), r=[kki, gi], w=[bT])
                    V(lambda e: e.tensor_tensor(out=kT[:], in0=k2[:], in1=gi[:], op=ALU.mult), r=[k2, gi], w=[kT])
                    for hh in range(2):
                        P = slice(hh * 64, hh * 64 + 64)
                        V(lambda e, hh=hh, P=P: e.tensor_tensor(out=rTp[P, hh, :], in0=zm[P, 0, :], in1=gm[P, :],
                                                                op=ALU.mult), r=[zm, gm], w=[rTp])
                    A(lambda e: e.copy(out=vb[:], in_=v_), r=[zm], w=[vb])
                    ptk = npb()
                    for k, (bb, ap) in enumerate([(bT, bT[:]), (kT, kT[:]), (vb, vb[:])]):
                        PE(lambda e, k=k, ap=ap: e.transpose(out=ptk[:, k * 128:(k + 1) * 128], in_=ap, identity=identb[:]),
                           r=[bb, identb], w=[ptk], last=(k == 2))
                    A(lambda e: e.copy(out=tok[:], in_=ptk[:, 0:384].rearrange("p (a b) -> p a b", b=128)),
                      r=[ptk], w=[tok])
                    pSU = npf()
                    for hh in range(2):
                        PE(lambda e, hh=hh: e.matmul(pSU[:, hh * 128:(hh + 1) * 128], lhsT=bT[:], rhs=aTp[:, hh, :],
                                                     start=True, stop=True), r=[bT, aTp], w=[pSU], last=False)
                        PE(lambda e, hh=hh: e.matmul(pSU[:, (2 + hh) * 128:(3 + hh) * 128], lhsT=kT[:], rhs=aTp[:, hh, :],
                                                     start=True, stop=True), r=[kT, aTp], w=[pSU], last=(hh == 1))
                    V(lambda e: e.tensor_tensor(out=SUm[:], in0=pSU[:].rearrange("p (a b) -> p a b", b=128), in1=mSU[:],
                                                op=ALU.mult), r=[pSU, mSU], w=[SUm])
                    pSL = npf()
                    for hh in range(2):
                        PE(lambda e, hh=hh: e.matmul(pSL[:, hh * 128:(hh + 1) * 128], lhsT=aTp[:, hh, :], rhs=bT[:],
                                                     start=True, stop=True), r=[aTp, bT], w=[pSL], last=(hh == 1))
                    V(lambda e: e.tensor_tensor(out=SLm[:], in0=pSL[:, 0:256].rearrange("p (a b) -> p a b", b=128),
                                                in1=mSL[:], op=ALU.mult), r=[pSL, mSL], w=[SLm])
                    pIU = npf()
                    for hh in range(2):
                        PE(lambda e, hh=hh: e.matmul(pIU[:, hh * 128:(hh + 1) * 128], lhsT=bT[:], rhs=rTp[:, hh, :],
                                                     start=True, stop=True), r=[bT, rTp], w=[pIU], last=False)
                        PE(lambda e, hh=hh: e.matmul(pIU[:, (2 + hh) * 128:(3 + hh) * 128], lhsT=kT[:], rhs=rTp[:, hh, :],
                                                     start=True, stop=True), r=[kT, rTp], w=[pIU], last=(hh == 1))
                    V(lambda e: e.tensor_tensor(out=IUm[:], in0=pIU[:].rearrange("p (a b) -> p a b", b=128), in1=mIU[:],
                                                op=ALU.mult), r=[pIU, mIU], w=[IUm])
                    V(lambda e: e.tensor_tensor(out=XT[1][:], in0=SUm[:, 0:2, :], in1=ident2[:], op=ALU.add),
                      r=[SUm, ident2], w=[XT[1]])
                    curP = (SLm, lambda hh: SLm[:, hh, :])
                    curPT = (SUm, lambda hh: SUm[:, hh, :])

                    def x_update(k, pk):
                        xo = XT[k % 2]
                        xn_ = XT[(k + 1) % 2]
                        px = npf()
                        for hh in range(2):
                            PE(lambda e, hh=hh: e.matmul(px[:, hh * 128:(hh + 1) * 128], lhsT=pk[1](hh),
                                                         rhs=xo[:, hh, :], start=True, stop=True),
                               r=[pk[0], xo], w=[px], last=(hh == 1))
                        V(lambda e: e.tensor_tensor(out=xn_[:], in0=px[:, 0:256].rearrange(
                            "p (a b) -> p a b", b=128), in1=xo[:], op=ALU.add), r=[px, xo], w=[xn_])

                    for lev in range(1, 7):
                        pq = npf()
                        for hh in range(2):
                            PE(lambda e, hh=hh, cp=curP, cpt=curPT: e.matmul(pq[:, hh * 128:(hh + 1) * 128],
                                                                             lhsT=cpt[1](hh), rhs=cp[1](hh),
                                                                             start=True, stop=True),
                               r=[curP[0], curPT[0]], w=[pq], last=(lev == 6 and hh == 1))
                        if lev < 6:
                            for hh in range(2):
                                PE(lambda e, hh=hh, cp=curP, cpt=curPT: e.matmul(pq[:, (2 + hh) * 128:(3 + hh) * 128],
                                                                                 lhsT=cp[1](hh), rhs=cpt[1](hh),
                                                                                 start=True, stop=True),
                                   r=[curP[0], curPT[0]], w=[pq], last=(hh == 1))
                        npa, npb_, npab = PP[lev % 2]
                        ev = A if lev % 2 == 0 else V
                        if lev < 6:
                            if ev is A:
                                A(lambda e, npab=npab: e.copy(out=npab[:], in_=pq[:].rearrange("p (a b) -> p a b", b=128)),
                                  r=[pq], w=[npab])
                            else:
                                V(lambda e, npab=npab: e.tensor_copy(out=npab[:],
                                                                     in_=pq[:].rearrange("p (a b) -> p a b", b=128)),
                                  r=[pq], w=[npab])
                        else:
                            A(lambda e, npa=npa: e.copy(out=npa[:], in_=pq[:, 0:256].rearrange("p (a b) -> p a b", b=128)),
                              r=[pq], w=[npa])
                        prevP = curP
                        curP = (npa, lambda hh, npa=npa: npa[:, hh, :])
                        curPT = (npb_, lambda hh, npb_=npb_: npb_[:, hh, :])
                        if lev >= 2:
                            x_update(lev - 1, prevP)
                    x_update(6, curP)
                    XTf = XT[1]

                    gC = gm[:, 127:128]
                    V(lambda e, c=c, gC=gC: e.tensor_scalar(out=Hg[:], in0=Hs[c][:], scalar1=gC, scalar2=None, op0=ALU.mult),
                      r=[Hs[c], gm], w=[Hg])
                    pW = npf()
                    for hh in range(2):
                        PE(lambda e, hh=hh, c=c: e.matmul(pW[:, hh * 64:(hh + 1) * 64], lhsT=aTp[:, hh, :], rhs=Hbd[c][:, hh, :],
                                                          start=True, stop=False), r=[aTp, Hbd[c]], w=[pW], last=False)
                        PE(lambda e, hh=hh: e.matmul(pW[:, hh * 64:(hh + 1) * 64], lhsT=SUm[:, 2 + hh, :],
                                                     rhs=tok[:, 2, hh * 64:(hh + 1) * 64], start=False, stop=True),
                           r=[SUm, tok], w=[pW], last=(hh == 1))
                    A(lambda e: e.copy(out=Wsb[:], in_=pW[:, 0:128]), r=[pW], w=[Wsb])
                    pU = npf()
                    for hh in range(2):
                        PE(lambda e, hh=hh, XTf=XTf: e.matmul(pU[:, hh * 64:(hh + 1) * 64], lhsT=XTf[:, hh, :],
                                                              rhs=Wsb[:, hh * 64:(hh + 1) * 64], start=True, stop=True),
                           r=[XTf, Wsb], w=[pU], last=(hh == 1))
                    V(lambda e: e.tensor_copy(out=Usb[:], in_=pU[:, 0:128]), r=[pU], w=[Usb])
                    pY = npf()
                    for hh in range(2):
                        P = slice(hh * 64, hh * 64 + 64)
                        PE(lambda e, hh=hh, P=P, c=c: e.matmul(pY[P, 0:128], lhsT=Hbd[c][:, hh, :], rhs=rTp[:, hh, :],
                                                               start=True, stop=False), r=[Hbd[c], rTp], w=[pY], last=False)
                        PE(lambda e, hh=hh, P=P: e.matmul(pY[P, 0:128], lhsT=Usb[:, hh * 64:(hh + 1) * 64],
                                                          rhs=IUm[:, hh, :], start=False, stop=False),
                           r=[Usb, IUm], w=[pY], last=False)
                        PE(lambda e, hh=hh, P=P: e.matmul(pY[P, 0:128], lhsT=tok[:, 2, hh * 64:(hh + 1) * 64],
                                                          rhs=IUm[:, 2 + hh, :], start=False, stop=True),
                           r=[tok, IUm], w=[pY], last=(hh == 1))
                    A(lambda e: e.copy(out=yt[:], in_=pY[:, 0:128]), r=[pY], w=[yt])
                    pH = npf()
                    for hh in range(2):
                        P = slice(hh * 64, hh * 64 + 64)
                        PE(lambda e, hh=hh, P=P: e.matmul(pH[P, 0:64], lhsT=tok[:, 0, hh * 64:(hh + 1) * 64],
                                                          rhs=Usb[:, hh * 64:(hh + 1) * 64], start=True, stop=False),
                           r=[tok, Usb], w=[pH], last=False)
                        PE(lambda e, hh=hh, P=P: e.matmul(pH[P, 0:64], lhsT=tok[:, 1, hh * 64:(hh + 1) * 64],
                                                          rhs=tok[:, 2, hh * 64:(hh + 1) * 64], start=False, stop=True),
                           r=[tok], w=[pH], last=(hh == 1))
                    V(lambda e, c=c, gC=gC: e.scalar_tensor_tensor(out=Hs[c][:], in0=pH[:, 0:64], scalar=gC, in1=Hg[:],
                                                                   op0=ALU.mult, op1=ALU.add), r=[pH, gm, Hg], w=[Hs[c]])
                    A(lambda e, c=c: e.copy(out=Hbd[c][0:64, 0, :], in_=Hs[c][0:64, :]), r=[Hs[c]], w=[Hbd[c]])
                    A(lambda e, c=c: e.copy(out=Hbd[c][64:128, 1, :], in_=Hs[c][64:128, :]), r=[Hs[c]], w=[Hbd[c]])

                    pm = npf()
                    PE(lambda e: e.matmul(pm[:, 0:128], lhsT=onesbd64[:], rhs=yt[:], start=True, stop=True),
                       r=[onesbd64, yt], w=[pm])
                    V(lambda e: e.tensor_tensor(out=cen[:], in0=yt[:], in1=pm[:, 0:128], op=ALU.subtract), r=[yt, pm], w=[cen])
                    A(lambda e: e.activation(out=sqc[:], in_=cen[:], func=AF.Square), r=[cen], w=[sqc])
                    pvv = npf()
                    PE(lambda e: e.matmul(pvv[:, 0:128], lhsT=onesbd64[:], rhs=sqc[:], start=True, stop=True),
                       r=[onesbd64, sqc], w=[pvv])
                    A(lambda e: e.activation(out=tv[:], in_=pvv[:, 0:128], func=AF.Ln, bias=GN_EPS), r=[pvv], w=[tv])
                    A(lambda e: e.activation(out=rsv[:], in_=tv[:], func=AF.Exp, scale=-0.5), r=[tv], w=[rsv])
                    V(lambda e: e.tensor_tensor(out=yn[:], in0=cen[:], in1=rsv[:], op=ALU.mult), r=[cen, rsv], w=[yn])
                    A(lambda e, c=c: e.activation(out=yn[:], in_=yn[:], func=AF.Identity, scale=rwpp[:, GG + c:GG + c + 1],
                                                  bias=rwpp[:, GB + c:GB + c + 1]), r=[yn, rwpp], w=[yn])
                    V(lambda e: e.tensor_tensor(out=o1[:], in0=o1[:], in1=yn[:], op=ALU.add), r=[o1, yn], w=[o1])
                    V(lambda e: e.tensor_tensor(out=slg[:], in0=g_, in1=sgate[:], op=ALU.mult), r=[zm, sgate], w=[slg])
                    if it > 0:
                        wait_flags([("D", it - 1)])
                    V(lambda e, c=c: e.tensor_tensor(out=yT[:, 4 + c, :], in0=o1[:], in1=slg[:], op=ALU.mult),
                      r=[o1, slg], w=[yT])
                    if nxt is not None:
                        c_n, need_xnT = nxt
                        if need_xnT:
                            wait_flags([("xnT",)])
                        PZ[si] = emit_z(c_n)
                        if not need_xnT:
                            zdone[("z", c_n)] = True

                nx2 = (lambda cn: (cn, True)) if it + 1 < ntile else (lambda cn: None)

                def t_a():
                    pair_body(0, SETS[0], 0, (2, False))
                    pair_body(2, SETS[0], 0, nx2(0))

                def t_b():
                    pair_body(1, SETS[1], 1, (3, False))
                    pair_body(3, SETS[1], 1, nx2(1))

                def t_c(pp=pending_post):
                    if pp is not None:
                        pp()
                    elif it == 0:
                        mem_phase()
                    if it == 0:
                        load_w_out()
                    if it + 1 < ntile:
                        pre_body(it + 1, [2, 3])
                pending_post = None
                fw.run_threads([in_pool("pA", t_a), in_pool("pB", t_b), in_pool("post", t_c)])
                def d_body(xt, it, last):
                    for blk in range(2):
                        p = npf()
                        for c in range(8):
                            PE(lambda e, c=c, p=p, blk=blk: e.matmul(p[:], lhsT=yT[:, c, :],
                                                                      rhs=w_out[:, c, blk * 512:(blk + 1) * 512],
                                                                      start=(c == 0), stop=(c == 7)),
                               r=[yT, w_out], w=[p], last=(c == 7))
                        V(lambda e, p=p, blk=blk: e.tensor_tensor(out=xt[:, blk * 512:(blk + 1) * 512],
                                                                  in0=xt[:, blk * 512:(blk + 1) * 512], in1=p[:],
                                                                  op=ALU.add), r=[xt, p], w=[xt])
                    zdone[("D", it)] = True
                    if not last:
                        finish_pre()
                h = xt
                ck('D')
                def post_body(h, r0, it):
                    rs3 = rstd_from_sumsq(h, h[:], D, RMS_EPS)
                    V(lambda e, rs3=rs3: e.tensor_scalar(out=hs[:], in0=h[:], scalar1=rs3[:, 0:1], scalar2=None, op0=ALU.mult),
                      r=[h, rs3], w=[hs])
                    transpose_to(hs, [hs[:, c * 128:(c + 1) * 128] for c in range(8)], hnT, hnT[:])
                    for half in range(2):
                        p = npf()
                        for jj in range(4):
                            j = half * 4 + jj
                            for c in range(8):
                                PE(lambda e, j=j, jj=jj, c=c, p=p: e.matmul(p[:, jj * 128:(jj + 1) * 128],
                                                                             lhsT=w_q[:, c, j * 128:(j + 1) * 128],
                                                                             rhs=hnT[:, c, :], start=(c == 0), stop=(c == 7)),
                                   r=[w_q, hnT], w=[p], last=(jj == 3 and c == 7))
                        A(lambda e, p=p, half=half: e.copy(out=qT[:, half * 4:(half + 1) * 4, :],
                                                           in_=p[:].rearrange("p (a b) -> p a b", b=128)), r=[p], w=[qT])
                    for half in range(2):
                        p = npf()
                        for hh in range(2):
                            h4 = half * 2 + hh
                            for jj in range(2):
                                PE(lambda e, h4=h4, hh=hh, jj=jj, p=p: e.matmul(p[:, hh * 256:(hh + 1) * 256],
                                                                                 lhsT=qT[:, 2 * h4 + jj, :],
                                                                                 rhs=KT[:, 2 * h4 + jj, :],
                                                                                 start=(jj == 0), stop=(jj == 1)),
                                   r=[qT, KT], w=[p], last=(hh == 1 and jj == 1))
                        V(lambda e, p=p, half=half: e.tensor_reduce(out=mx[:, half * 2:half * 2 + 2],
                                                                    in_=p[:].rearrange("p (a b) -> p a b", b=256),
                                                                    axis=AX.X, op=ALU.max), r=[p], w=[mx])
                        V(lambda e, half=half: e.tensor_scalar(out=mx[:, 4 + half * 2:6 + half * 2],
                                                               in0=mx[:, half * 2:half * 2 + 2], scalar1=-1.0, scalar2=None,
                                                               op0=ALU.mult), r=[mx], w=[mx])
                        for hh in range(2):
                            h4 = half * 2 + hh
                            A(lambda e, p=p, hh=hh, h4=h4: e.activation(out=Ee[:, h4, :], in_=p[:, hh * 256:(hh + 1) * 256],
                                                                        func=AF.Exp, bias=mx[:, 4 + h4:5 + h4],
                                                                        accum_out=rsum[:, h4:h4 + 1]),
                              r=[p, mx], w=[Ee, rsum])
                    V(lambda e: e.reciprocal(out=rsum[:, 4:8], in_=rsum[:, 0:4]), r=[rsum], w=[rsum])
                    for h4 in range(4):
                        V(lambda e, h4=h4: e.tensor_scalar(out=Ee[:, h4, :], in0=Ee[:, h4, :], scalar1=rsum[:, 4 + h4:5 + h4],
                                                           scalar2=None, op0=ALU.mult), r=[Ee, rsum], w=[Ee])
                    transpose_to(Ee, [Ee[:, k // 2, (k % 2) * 128:(k % 2 + 1) * 128] for k in range(8)], PT, PT[:])
                    for half in range(2):
                        p = npf()
                        for jj in range(4):
                            j = half * 4 + jj
                            h4, dd = j // 2, j % 2
                            for mc in range(2):
                                PE(lambda e, jj=jj, h4=h4, dd=dd, mc=mc, p=p: e.matmul(
                                    p[:, jj * 128:(jj + 1) * 128],
                                    lhsT=Vm[:, mc, h4 * 256 + dd * 128:h4 * 256 + (dd + 1) * 128],
                                    rhs=PT[:, h4 * 2 + mc, :], start=(mc == 0), stop=(mc == 1)),
                                   r=[Vm, PT], w=[p], last=(jj == 3 and mc == 1))
                        A(lambda e, p=p, half=half: e.copy(out=oT[:, half * 4:(half + 1) * 4, :],
                                                           in_=p[:].rearrange("p (a b) -> p a b", b=128)), r=[p], w=[oT])
                    for blk in range(2):
                        p = npf()
                        for c in range(8):
                            PE(lambda e, c=c, p=p, blk=blk: e.matmul(p[:], lhsT=oT[:, c, :],
                                                                      rhs=w_o[:, c, blk * 512:(blk + 1) * 512],
                                                                      start=(c == 0), stop=(c == 7)),
                               r=[oT, w_o], w=[p], last=(c == 7))
                        V(lambda e, p=p, blk=blk: e.tensor_tensor(out=h[:, blk * 512:(blk + 1) * 512],
                                                                  in0=h[:, blk * 512:(blk + 1) * 512], in1=p[:], op=ALU.add),
                          r=[h, p], w=[h])
                    rs4 = rstd_from_sumsq(h, h[:], D, RMS_EPS)
                    ob = ot[it % 2]
                    V(lambda e, rs4=rs4, ob=ob: e.scalar_tensor_tensor(out=ob[:], in0=h[:], scalar=rs4[:, 0:1], in1=lnf_b[:],
                                                                       op0=ALU.mult, op1=ALU.mult), r=[h, rs4, lnf_b], w=[ob])
                    fw.dma("sync", lambda e, ob=ob, r0=r0: e.dma_start(out=out_d[r0:r0 + 128, :], in_=ob[:]), ob, reads=[ob])
                    fw.mark_out(ob)
                if it == 0:
                    def post0(h=h, r0=r0, post_body=post_body, d_body=d_body):
                        d_body(h, 0, ntile == 1)
                        load_w_q()
                        load_w_o()
                        post_body(h, r0, 0)
                    pending_post = post0
                else:
                    def postn(h=h, r0=r0, it=it, post_body=post_body, d_body=d_body):
                        d_body(h, it, it + 1 >= ntile)
                        post_body(h, r0, it)
                    pending_post = postn

            if pending_post is not None:
                tls.pool = "post"
                pending_post()

        try:
            record()
        except StopBuild:
            print('stopped at', stop)
        fw.build(st)
        ninst = {k: len(v.prog) for k, v in fw.E.items()}
        print("program entries:", ninst)
    return nc


_NC = None


def kernel(x, mem, ln_mix_g, w_in, sgu_ln_g, sgu_ln_b, sgu_ws, sgu_bs, sgu_out_g,
           rw_mu, rw_w0, rw_w2, rw_a0, rw_a2, rw_k_k, rw_k_a, rw_r_k, rw_gn_g, rw_gn_b,
           w_out, ln_x_g, ln_mem_g, w_q, w_kv, w_o, ln_f_g):
    global _NC
    f = lambda a: np.ascontiguousarray(np.asarray(a, dtype=np.float32))
    pp = lambda v, n: f(np.asarray(v, dtype=np.float32).reshape(n, 128).T)
    gpp = np.zeros((128, 32), np.float32)
    gpp[:, 0:8] = pp(ln_mix_g[0], 8)
    gpp[:, 8:16] = pp(ln_x_g[0], 8)
    gpp[:, 16:24] = pp(ln_mem_g[0], 8)
    gpp[:, 24:28] = pp(sgu_out_g[0], 4)
    rwpp = np.zeros((128, 48), np.float32)
    rwpp[:, 0:17] = pp(rw_mu[0], 17)
    rwpp[:, 17:21] = pp(rw_w0[0], 4)
    rwpp[:, 21:25] = pp(rw_a0[0], 4)
    rwpp[:, 25:29] = pp(rw_k_k[0], 4)
    rwpp[:, 29:33] = pp(rw_k_a[0], 4)
    rwpp[:, 33:37] = pp(np.asarray(rw_r_k[0]).reshape(-1), 4)
    rwpp[:, 37:41] = pp(rw_gn_g[0], 4)
    rwpp[:, 41:45] = pp(rw_gn_b[0], 4)
    shared = {
        "w_in": f(w_in[0]), "w_out": f(w_out[0]), "w_q": f(w_q[0]), "w_kv": f(w_kv[0]), "w_o": f(w_o[0]),
        "gpp": gpp, "lnf": f(np.asarray(ln_f_g).reshape(1, D)), "lng": f(np.asarray(sgu_ln_g[0]).reshape(1, 512)),
        "lnb": f(np.asarray(sgu_ln_b[0]).reshape(1, 512)),
        "wsT": f(np.transpose(np.asarray(sgu_ws[0]), (0, 2, 1))),
        "bsT": f(np.asarray(sgu_bs[0]).T), "rwpp": rwpp, "w2": f(rw_w2[0]), "a2": f(rw_a2[0]),
    }
    if _NC is None:
        _NC = build_program()
    x = np.asarray(x, dtype=np.float32)
    mem = np.asarray(mem, dtype=np.float32)
    in_maps = []
    for b in range(8):
        m = dict(shared)
        m["x"] = f(x[b])
        m["mem"] = f(mem[b])
        in_maps.append(m)
    res = run_bass_kernel_spmd(_NC, in_maps, core_ids=list(range(8)))
    out = np.stack([np.asarray(r["out"], dtype=np.float32) for r in res.results], axis=0)
    return out
```

```python
import math
import threading
import numpy as np
from contextlib import ExitStack
import concourse.bass as bass
import concourse.mybir as mybir
from concourse.bass_utils import run_bass_kernel_spmd

F32 = mybir.dt.float32
BF16 = mybir.dt.bfloat16
AF = mybir.ActivationFunctionType
ALU = mybir.AluOpType
AX = mybir.AxisListType

S = 2048
D = 1024
NTILE = 16
CIN = 3712
C0 = math.exp(-0.5)
RMS_EPS = 1e-6
LN_EPS = 1e-5
GN_EPS = 64e-5


class Eng:
    def __init__(self, name):
        self.name = name
        self.prog = []
        self.count = 0
        self.seen = {}
        self.sem = None


class Buf:
    _n = 0

    def __init__(self, t, name=None):
        self.t = t
        self.name = name
        self.w = None
        self.r = []
        self.dsem = None
        self.dcnt = 0
        self.alias = []
        self.xd = []
        self.lo = self.hi = 0
        Buf._n += 1

    def __getitem__(self, k):
        return self.t[k]


class Rec:
    def __init__(self):
        self.call = None

    def __getattr__(self, name):
        def f(*a, **k):
            self.call = (name, a, k)
            return self
        return f


class FW:
    SYNC_LAT = 0.25
    AUX_BIAS = 0.15

    def __init__(self, nc):
        self.nc = nc
        self.E = {n: Eng(n) for n in ("sync", "scalar", "vector", "gpsimd", "tensor")}
        self.dma_bufs = []
        self.out_bufs = []
        self.hook = None
        self.arb = None
        self.atomic = False
        self.efree = {n: 0.0 for n in self.E}
        self.fin = {}
        self.dfin = {}

    def _need(self, eng, key, val):
        if eng.seen.get(key, 0) < val:
            eng.seen[key] = val
            eng.prog.append(("wait", key, val))

    def _collect(self, eng, reads, writes):
        pe = eng.name == "tensor"
        out = []
        for b in reads:
            if b.w is not None:
                e2, s = b.w
                if not (e2 is eng and pe):
                    out.append(("eng", e2.name, s))
            for a in b.alias:
                if a.w is not None and not (a.w[0] is eng and pe):
                    out.append(("eng", a.w[0].name, a.w[1]))
            if b.dcnt:
                out.append(("dma", id(b), b.dcnt))
            for t in b.xd:
                out.append(("dma", id(t), t.dcnt))
        for b in writes:
            for t in b.xd:
                out.append(("dma", id(t), t.dcnt))
            for a in b.alias:
                if a.w is not None:
                    out.append(("eng", a.w[0].name, a.w[1]))
                for (e2, s) in a.r:
                    out.append(("eng", e2.name, s))
            if b.w is not None:
                e2, s = b.w
                if not (e2 is eng and pe):
                    out.append(("eng", e2.name, s))
            for (e2, s) in b.r:
                if not (e2 is eng and pe):
                    out.append(("eng", e2.name, s))
            if b.dcnt:
                out.append(("dma", id(b), b.dcnt))
        return out

    def _deps(self, eng, reads, writes):
        for d in self._collect(eng, reads, writes):
            if d[0] == "eng":
                self._need(eng, ("eng", d[1]), d[2])
            else:
                self._need(eng, ("dma", d[1]), d[2])

    def _ready(self, ename, reads, writes):
        eng = self.E[ename]
        t = self.efree[ename]
        for d in self._collect(eng, reads, writes):
            if d[0] == "eng":
                f = self.fin.get((d[1], d[2]))
                if f is None:
                    f = self.efree[d[1]]
                f += 0.08 if d[1] == ename else self.SYNC_LAT
            else:
                f = self.dfin.get(d[1], 0.0)
            if f > t:
                t = f
        return t

    @staticmethod
    def _cost(ename, call):
        nm, a, k = call
        ap = k.get("out", a[0] if a else None)
        try:
            n = ap.free_size()
        except Exception:
            n = 128
        if ename == "tensor":
            if nm == "transpose":
                return 0.09
            r = k.get("rhs")
            try:
                n = r.free_size()
                f32 = (r.dtype == F32)
            except Exception:
                n, f32 = 128, False
            c = max(0.055, n / 2400.0 + 0.02)
            return c * 4 if f32 else c
        if ename == "vector":
            c = 0.12 + n / 960.0
            return c * 1.8 if nm == "tensor_tensor_scan" else c
        if ename == "scalar":
            return 0.2 + n / 1200.0
        if ename == "gpsimd":
            return 1.0 + n / 120.0
        return 0.05

    def op(self, ename, fn, reads=(), writes=(), inc=True):
        eng = self.E[ename]
        rec = Rec()
        fn(rec)
        if self.arb is not None:
            self.arb(("op", ename, reads, writes))
        start = self._ready(ename, reads, writes)
        self._deps(eng, reads, writes)
        if inc:
            eng.count += 1
            seq = eng.count
        else:
            seq = eng.count + 1
        eng.prog.append(("op", rec.call, inc))
        endt = start + self._cost(ename, rec.call)
        self.efree[ename] = endt if ename != "tensor" else start + max(0.055, self._cost(ename, rec.call))
        self.fin[(ename, seq)] = endt + (0.12 if ename == "tensor" else 0.0)
        for b in reads:
            b.r.append((eng, seq))
            if len(b.r) > 64:
                mx = {}
                for (e2, s) in b.r:
                    if mx.get(e2.name, (None, 0))[1] < s:
                        mx[e2.name] = (e2, s)
                b.r = list(mx.values())
        for b in writes:
            b.w = (eng, seq)
            b.r = []
        return seq

    def dma(self, ename, fn, buf, reads=(), writes=(), oneshot=False):
        eng = self.E[ename]
        rec = Rec()
        fn(rec)
        if self.arb is not None:
            self.arb(("op", ename, reads, writes))
        start = self._ready(ename, reads, writes)
        self._deps(eng, reads, writes)
        if oneshot:
            tokb = Buf(None, "tok")
            for b in list(reads) + list(writes):
                b.xd.append(tokb)
            buf = tokb
        if buf.dsem is None:
            buf.dsem = True
            self.dma_bufs.append(buf)
        buf.dcnt += 16
        eng.prog.append(("dma", rec.call, buf))
        self.efree[ename] = start + 0.1
        self.dfin[id(buf)] = start + 4.5
        for b in writes:
            b.w = None
            b.r = []

    def run_threads(self, fns):
        n = len(fns)
        if n == 1:
            fns[0]()
            return
        sems = [threading.Semaphore(0) for _ in range(n)]
        done = [False] * n
        prop = [None] * n
        main = threading.Semaphore(0)
        errs = []
        tl = threading.local()
        rr = [0]

        def choose():
            alive = [j for j in range(n) if not done[j]]
            if not alive:
                return None
            for j in alive:
                if prop[j] is None:
                    return j
            best, bt = None, None
            for kk_ in range(n):
                j = (rr[0] + 1 + kk_) % n
                if done[j]:
                    continue
                p = prop[j]
                t = 1e18 if p[0] == "wait" else self._ready(p[1], p[2], p[3])
                if bt is None or t < bt - 1e-9:
                    best, bt = j, t
            rr[0] = best
            return best

        def arb(desc):
            i = getattr(tl, "idx", None)
            if i is None or (self.atomic and desc[0] != "wait"):
                return
            prop[i] = desc
            j = choose()
            if j != i:
                sems[j].release()
                sems[i].acquire()

        def hook():
            arb(("wait",))

        def wrapper(i):
            sems[i].acquire()
            tl.idx = i
            try:
                if not errs:
                    fns[i]()
            except BaseException as ex:
                errs.append(ex)
            finally:
                done[i] = True
                j = choose()
                if j is not None:
                    sems[j].release()
                else:
                    main.release()

        ths = [threading.Thread(target=wrapper, args=(i,)) for i in range(n)]
        for t in ths:
            t.start()
        old = (self.hook, self.arb)
        self.hook, self.arb = hook, arb
        sems[0].release()
        main.acquire()
        for t in ths:
            t.join()
        self.hook, self.arb = old
        if errs:
            raise errs[0]

    def mark_out(self, buf):
        if buf not in self.out_bufs:
            self.out_bufs.append(buf)

    def build(self, stack):
        nc = self.nc
        semmap = {}
        for e in self.E.values():
            e.sem = stack.enter_context(nc.semaphore("s_" + e.name))
            semmap[("eng", e.name)] = e.sem
        for i, b in enumerate(self.dma_bufs):
            b.dsem = stack.enter_context(nc.semaphore("d%d" % i))
            semmap[("dma", id(b))] = b.dsem
        sy = self.E["sync"]
        for b in self.out_bufs:
            self._need(sy, ("dma", id(b)), b.dcnt)
        block = stack.enter_context(nc.Block())
        handles = {"sync": nc.sync, "scalar": nc.scalar, "vector": nc.vector,
                   "gpsimd": nc.gpsimd, "tensor": nc.tensor}

        def run(e):
            h = handles[e.name]
            for ent in e.prog:
                if ent[0] == "wait":
                    h.wait_ge(semmap[ent[1]], ent[2])
                elif ent[0] == "op":
                    nm, a, k = ent[1]
                    ins = getattr(h, nm)(*a, **k)
                    if ent[2]:
                        ins.then_inc(e.sem, 1)
                else:
                    nm, a, k = ent[1]
                    ins = getattr(h, nm)(*a, **k)
                    ins.then_inc(ent[2].dsem, 16)

        @block.sync
        def _(x):
            run(self.E["sync"])

        @block.scalar
        def _(x):
            run(self.E["scalar"])

        @block.vector
        def _(x):
            run(self.E["vector"])

        @block.gpsimd
        def _(x):
            run(self.E["gpsimd"])

        @block.tensor
        def _(x):
            run(self.E["tensor"])


def build_program(ntile=NTILE, stop=None):
    nc = bass.Bass("TRN2", target_bir_lowering=False)

    def din(name, shape):
        return nc.dram_tensor(name, shape, F32, kind="ExternalInput").ap()

    x_d = din("x", [S, D])
    mem_d = din("mem", [256, D])
    w_in_d = din("w_in", [D, CIN])
    w_out_d = din("w_out", [D, D])
    w_q_d = din("w_q", [D, D])
    w_kv_d = din("w_kv", [D, 2 * D])
    w_o_d = din("w_o", [D, D])
    gpp_d = din("gpp", [128, 32])
    lnf_d = din("lnf", [1, D])
    lng_d = din("lng", [1, 512])
    lnb_d = din("lnb", [1, 512])
    wsT_d = din("wsT", [4, 128, 128])
    bsT_d = din("bsT", [128, 4])
    rwpp_d = din("rwpp", [128, 48])
    w2_d = din("w2", [64, 512])
    a2_d = din("a2", [64, 512])
    out_d = nc.dram_tensor("out", [S, D], F32, kind="ExternalOutput").ap()

    st = ExitStack()
    with st:
        fw = FW(nc)

        class StopBuild(Exception):
            pass

        def ck(name):
            if stop == name:
                raise StopBuild()

        def record():

            def sb(shape, dt=F32, name=None):
                nm = "sb_" + (name or ("t%d" % Buf._n))
                return Buf(st.enter_context(nc.sbuf_tensor(nm, shape, dt)), nm)

            def psb(shape, dt=F32, name=None):
                nm = "ps_" + (name or ("p%d" % Buf._n))
                return Buf(st.enter_context(nc.psum_tensor(nm, shape, dt)), nm)

            V = lambda fn, r=(), w=(): fw.op("vector", fn, r, w)
            A = lambda fn, r=(), w=(): fw.op("scalar", fn, r, w)
            G = lambda fn, r=(), w=(): fw.op("gpsimd", fn, r, w)
            PE = lambda fn, r=(), w=(), last=True: fw.op("tensor", fn, r, w, inc=last)

            pf, pb = [], []
            for i in range(8):
                th_ = st.enter_context(nc.psum_tensor("ps_bank%d" % i, [128, 512], F32))
                f_ = Buf(th_, "pf%d" % i)
                b_ = Buf(th_[:].bitcast(BF16), "pb%d" % i)
                f_.alias = [b_]
                b_.alias = [f_]
                pf.append(f_)
                pb.append(b_)

            class Pool:
                def __init__(self, fidx, bidx):
                    self.f = fidx
                    self.b = bidx
                    self.fi = 0
                    self.bi = 0

            POOLS = {"main": Pool(list(range(8)), list(range(8))),
                     "pA": Pool([0, 1], [2]), "pB": Pool([3, 4], [5]), "post": Pool([6, 7], [6, 7])}
            tls = threading.local()

            def cur_pool():
                return POOLS[getattr(tls, "pool", "main")]

            def npf():
                p_ = cur_pool()
                b = pf[p_.f[p_.fi % len(p_.f)]]
                p_.fi += 1
                return b

            def npb():
                p_ = cur_pool()
                if p_.b is p_.f or p_.b == p_.f:
                    b = pb[p_.f[p_.fi % len(p_.f)]]
                    p_.fi += 1
                else:
                    b = pb[p_.b[p_.bi % len(p_.b)]]
                    p_.bi += 1
                return b

            def in_pool(name, fn):
                def g():
                    tls.pool = name
                    fn()
                return g

            ones512 = sb([128, 128], F32, "ones128")
            G(lambda e: e.memset(ones512[:], 1.0), w=[ones512])
            identf = sb([128, 128], F32, "identf")
            G(lambda e: e.affine_select(out=identf[:], in_=ones512[:, 0:128], pattern=[[-1, 128]],
                                        compare_op=ALU.is_equal, fill=0.0, base=0, channel_multiplier=1),
              r=[ones512], w=[identf])
            identb = sb([128, 128], BF16, "identb")
            G(lambda e: e.tensor_copy(out=identb[:], in_=identf[:]), r=[identf], w=[identb])
            ident2 = sb([128, 2, 128], BF16, "ident2")
            for k in range(2):
                G(lambda e, k=k: e.tensor_copy(out=ident2[:, k, :], in_=identf[:]), r=[identf], w=[ident2])
            mSU = sb([128, 4, 128], BF16, "mSU")
            mIU = sb([128, 4, 128], BF16, "mIU")
            mSL = sb([128, 2, 128], BF16, "mSL")
            for (mk, cmp_, cm, stp, ncp) in [(mSU, ALU.is_gt, -1, 1, 4), (mIU, ALU.is_ge, -1, 1, 4), (mSL, ALU.is_gt, 1, -1, 2)]:
                G(lambda e, mk=mk, cmp_=cmp_, cm=cm, stp=stp: e.affine_select(
                    out=mk[:, 0, :], in_=ones512[:], pattern=[[stp, 128]], compare_op=cmp_, fill=0.0, base=0,
                    channel_multiplier=cm), r=[ones512], w=[mk])
                for k in range(1, ncp):
                    G(lambda e, mk=mk, k=k: e.tensor_copy(out=mk[:, k, :], in_=mk[:, 0, :]), r=[mk], w=[mk])
            onesbd = sb([128, 128], F32, "onesbd")
            G(lambda e: e.memset(onesbd[:], 0.0), w=[onesbd])
            G(lambda e: e.memset(onesbd[0:64, 0:64], 1.0), w=[onesbd])
            G(lambda e: e.memset(onesbd[64:128, 64:128], 1.0), w=[onesbd])
            onesbd64 = sb([128, 128], F32, "onesbd64")
            G(lambda e: e.tensor_scalar(out=onesbd64[:], in0=onesbd[:], scalar1=1.0 / 64.0, scalar2=None,
                                        op0=ALU.mult), r=[onesbd], w=[onesbd64])

            ck('consts')
            gpp = sb([128, 32], F32, "gpp")
            fw.dma("sync", lambda e: e.dma_start(out=gpp[:], in_=gpp_d), gpp, writes=[gpp])
            rwpp = sb([128, 48], F32, "rwpp")
            fw.dma("sync", lambda e: e.dma_start(out=rwpp[:], in_=rwpp_d), rwpp, writes=[rwpp])
            bsT = sb([128, 4], F32, "bsT")
            fw.dma("sync", lambda e: e.dma_start(out=bsT[:], in_=bsT_d), bsT, writes=[bsT])
            lnf_b = sb([128, D], F32, "lnf_b")
            fw.dma("sync", lambda e: e.dma_start(out=lnf_b[:], in_=lnf_d.partition_broadcast(128)), lnf_b, writes=[lnf_b])
            lng_b = sb([128, 512], F32, "lng_b")
            fw.dma("sync", lambda e: e.dma_start(out=lng_b[:], in_=lng_d.partition_broadcast(128)), lng_b, writes=[lng_b])
            lnb_b = sb([128, 512], F32, "lnb_b")
            fw.dma("sync", lambda e: e.dma_start(out=lnb_b[:], in_=lnb_d.partition_broadcast(128)), lnb_b, writes=[lnb_b])
            w2b = sb([128, 512], BF16, "w2b")
            G(lambda e: e.memset(w2b[:], 0.0), w=[w2b])
            fw.dma("gpsimd", lambda e: e.dma_start(out=w2b[0:64, :], in_=w2_d), w2b, writes=[w2b])
            a2b = sb([128, 512], BF16, "a2b")
            G(lambda e: e.memset(a2b[:], 0.0), w=[a2b])
            fw.dma("gpsimd", lambda e: e.dma_start(out=a2b[64:128, :], in_=a2_d), a2b, writes=[a2b])
            wsTm = sb([128, 4, 128], BF16, "wsTm")
            nrw = sb([128, 8], F32, "nrw")
            V(lambda e: e.tensor_scalar(out=nrw[:], in0=rwpp[:, 17:25], scalar1=-1.0, scalar2=None, op0=ALU.mult),
              r=[rwpp], w=[nrw])
            MU, W0, A0, KK, KA, RK, GG, GB = 0, 17, 21, 25, 29, 33, 37, 41

            ck('params')
            stage = [sb([128, 1024], F32, "stage%d" % i) for i in range(2)]
            sti = [0]
            cast_eng = ["vector", "scalar"]

            def load_w(dst, dst_ap, src_ap, ncol, scale_ap, stages=None):
                stl = stages if stages is not None else stage
                sg = stl[sti[0] % len(stl)]
                eng = cast_eng[sti[0] % 2]
                sti[0] += 1
                fw.dma("sync", lambda e: e.dma_start(out=sg[:, 0:ncol], in_=src_ap), sg, writes=[sg])
                if eng == "scalar":
                    if scale_ap is None:
                        fw.op(eng, lambda e: e.copy(out=dst_ap, in_=sg[:, 0:ncol]), [sg], [dst])
                    else:
                        fw.op(eng, lambda e: e.mul(out=dst_ap, in_=sg[:, 0:ncol], mul=scale_ap), [sg, gpp], [dst])
                elif scale_ap is None:
                    fw.op(eng, lambda e: e.tensor_copy(out=dst_ap, in_=sg[:, 0:ncol]), [sg], [dst])
                else:
                    fw.op(eng, lambda e: e.tensor_scalar(out=dst_ap, in0=sg[:, 0:ncol], scalar1=scale_ap,
                                                         scalar2=None, op0=ALU.mult), [sg, gpp], [dst])

            sg0 = stage[0]
            fw.dma("sync", lambda e: e.dma_start(out=sg0[:, 0:512].rearrange("p (a b) -> p a b", b=128),
                                                 in_=wsT_d.rearrange("h s t -> s h t")), sg0, writes=[sg0])
            V(lambda e: e.tensor_tensor(out=wsTm[:], in0=sg0[:, 0:512].rearrange("p (a b) -> p a b", b=128),
                                        in1=mIU[:], op=ALU.mult), r=[sg0, mIU], w=[wsTm])

            w_in = sb([128, 8, CIN], BF16, "w_in")
            w_in_r = Buf(w_in.t, "w_in_r")
            for q2 in range(2):
                for c in range(8):
                    c0 = q2 * 768
                    load_w(w_in, w_in[:, c, c0:c0 + 768], w_in_d[c * 128:(c + 1) * 128, c0:c0 + 768], 768,
                           gpp[:, c:c + 1])

            def load_w_in_r():
                for (c0, nn) in [(1536, 726), (2262, 725), (2987, 725)]:
                    for c in range(8):
                        load_w(w_in_r, w_in[:, c, c0:c0 + nn], w_in_d[c * 128:(c + 1) * 128, c0:c0 + nn], nn,
                               gpp[:, c:c + 1])
                zdone[("wr",)] = True
            wkv_blk = sb([128, 8, 512], BF16, "wkv_blk")
            w_out = sb([128, 8, D], BF16, "w_out")
            w_q = sb([128, 8, D], BF16, "w_q")
            w_o = sb([128, 8, D], BF16, "w_o")

            def load_w_out():
                for c in range(8):
                    load_w(w_out, w_out[:, c, :], w_out_d[c * 128:(c + 1) * 128, :], 1024,
                           gpp[:, 24 + c:25 + c] if c < 4 else None)

            def load_w_q():
                for c in range(8):
                    load_w(w_q, w_q[:, c, :], w_q_d[c * 128:(c + 1) * 128, :], 1024, gpp[:, 8 + c:9 + c])

            def load_w_o():
                for c in range(8):
                    load_w(w_o, w_o[:, c, :], w_o_d[c * 128:(c + 1) * 128, :], 1024, None)

            ck('w_in')
            junk = sb([128, D], BF16, "junk")
            st1 = [sb([128, 4], F32, "st1_%d" % i) for i in range(8)]
            sti1 = [0]

            def nst():
                b = st1[sti1[0] % 8]
                sti1[0] += 1
                return b

            def rstd_from_sumsq(src_buf, src_ap, n, eps):
                ssq = nst()
                A(lambda e: e.activation(out=junk[:, 0:n], in_=src_ap, func=AF.Square, accum_out=ssq[:, 0:1]),
                  r=[src_buf], w=[junk, ssq])
                A(lambda e: e.activation(out=ssq[:, 1:2], in_=ssq[:, 0:1], func=AF.Ln, scale=1.0 / n, bias=eps),
                  r=[ssq], w=[ssq])
                rs = nst()
                A(lambda e: e.activation(out=rs[:, 0:1], in_=ssq[:, 1:2], func=AF.Exp, scale=-0.5), r=[ssq], w=[rs])
                return rs

            def sigmoid3(out_buf, out_ap, in_buf, in_ap, nbias=None, nbias_buf=None):
                rr = [in_buf] + ([nbias_buf] if nbias_buf is not None else [])
                if nbias is None:
                    A(lambda e: e.activation(out=out_ap, in_=in_ap, func=AF.Exp, scale=-1.0), r=rr, w=[out_buf])
                else:
                    A(lambda e: e.activation(out=out_ap, in_=in_ap, func=AF.Exp, scale=-1.0, bias=nbias),
                      r=rr, w=[out_buf])
                A(lambda e: e.activation(out=out_ap, in_=out_ap, func=AF.Ln, bias=1.0), r=[out_buf], w=[out_buf])
                A(lambda e: e.activation(out=out_ap, in_=out_ap, func=AF.Exp, scale=-1.0), r=[out_buf], w=[out_buf])

            def transpose_to(src_buf, src_aps, dst_buf, dst_ap3):
                n = len(src_aps)
                p = npb()
                for k, ap in enumerate(src_aps):
                    PE(lambda e, k=k, ap=ap: e.transpose(out=p[:, k * 128:(k + 1) * 128], in_=ap, identity=identb[:]),
                       r=[src_buf, identb], w=[p], last=(k == n - 1))
                A(lambda e: e.copy(out=dst_ap3, in_=p[:, 0:n * 128].rearrange("p (a b) -> p a b", b=128)),
                  r=[p], w=[dst_buf])

            ck('scratch')
            mnT = sb([128, 8, 256], BF16, "mnT")
            KT = sb([128, 8, 256], BF16, "KT")
            Vm = sb([128, 2, D], BF16, "Vm")
            xt_bufs = [sb([128, D], F32, "xt%d" % i) for i in range(2)]
            xs = sb([128, D], BF16, "xs")

            def mem_phase():
                for mt in range(2):
                    mtile = stage[0]
                    fw.dma("sync", lambda e, mt=mt, mtile=mtile: e.dma_start(out=mtile[:], in_=mem_d[mt * 128:(mt + 1) * 128, :]),
                           mtile, writes=[mtile])
                    rs = rstd_from_sumsq(mtile, mtile[:], D, RMS_EPS)
                    V(lambda e, mtile=mtile, rs=rs: e.tensor_scalar(out=hs[:], in0=mtile[:], scalar1=rs[:, 0:1], scalar2=None,
                                                                    op0=ALU.mult), r=[mtile, rs], w=[hs])
                    transpose_to(hs, [hs[:, c * 128:(c + 1) * 128] for c in range(8)], mnT,
                                 mnT[:, :, mt * 128:(mt + 1) * 128])
                for b4 in range(4):
                    for c in range(8):
                        load_w(wkv_blk, wkv_blk[:, c, :], w_kv_d[c * 128:(c + 1) * 128, b4 * 512:(b4 + 1) * 512], 512,
                               gpp[:, 16 + c:17 + c], stages=[stage[0], stage[1], xt_bufs[1]])
                    if b4 < 2:
                        for jj in range(4):
                            j = b4 * 4 + jj
                            p = npf()
                            for c in range(8):
                                PE(lambda e, jj=jj, c=c, p=p: e.matmul(p[:, 0:256], lhsT=wkv_blk[:, c, jj * 128:(jj + 1) * 128],
                                                                       rhs=mnT[:, c, :], start=(c == 0), stop=(c == 7)),
                                   r=[wkv_blk, mnT], w=[p], last=(c == 7))
                            A(lambda e, j=j, p=p: e.mul(out=KT[:, j, :], in_=p[:, 0:256], mul=1.0 / 16.0), r=[p], w=[KT])
                    else:
                        blk = b4 - 2
                        for mt in range(2):
                            p = npf()
                            for c in range(8):
                                PE(lambda e, mt=mt, c=c, p=p: e.matmul(p[:], lhsT=mnT[:, c, mt * 128:(mt + 1) * 128],
                                                                       rhs=wkv_blk[:, c, :], start=(c == 0), stop=(c == 7)),
                                   r=[wkv_blk, mnT], w=[p], last=(c == 7))
                            V(lambda e, mt=mt, blk=blk, p=p: e.tensor_copy(out=Vm[:, mt, blk * 512:(blk + 1) * 512], in_=p[:]),
                              r=[p], w=[Vm])


            ck('mem')
            xnT = sb([128, 8, 128], BF16, "xnT")
            yT = sb([128, 8, 128], BF16, "yT")
            bn6 = sb([128, 6], F32, "bn6")
            mv = sb([128, 2], F32, "mv")
            carry = sb([128, 20], F32, "carry")
            G(lambda e: e.memset(carry[:], 0.0), w=[carry])
            Hs = [sb([128, 64], F32, "Hs%d" % c) for c in range(4)]
            Hbd = [sb([128, 2, 64], BF16, "Hbd%d" % c) for c in range(4)]
            for c in range(4):
                G(lambda e, c=c: e.memset(Hs[c][:], 0.0), w=[Hs[c]])
                G(lambda e, c=c: e.memset(Hbd[c][:], 0.0), w=[Hbd[c]])
            mx = sb([128, 8], F32, "mx")
            rsum = sb([128, 8], F32, "rsum")

            RWORDS = 8800
            region = st.enter_context(nc.sbuf_tensor("sb_region", [128, RWORDS], F32))
            views = []

            def nwords(shape, dt):
                n = 1
                for d_ in shape[1:]:
                    n *= d_
                return n if dt == F32 else (n + 1) // 2

            def rv(name, off_w, shape, dt=F32):
                nw = nwords(shape, dt)
                assert off_w + nw <= RWORDS, (name, off_w, nw)
                ap = region[:, off_w:off_w + nw]
                if dt != F32:
                    ap = ap.bitcast(dt)
                if len(shape) == 3:
                    ap = ap.rearrange("p (a b) -> p a b", b=shape[2])
                vb_ = Buf(ap, name)
                vb_.lo, vb_.hi = off_w, off_w + nw
                for o in views:
                    if o.lo < vb_.hi and vb_.lo < o.hi:
                        o.alias.append(vb_)
                        vb_.alias.append(o)
                views.append(vb_)
                return vb_

            class Alloc:
                def __init__(self, base):
                    self.off = base

                def __call__(self, name, shape, dt=F32):
                    v_ = rv(name, self.off, shape, dt)
                    self.off += nwords(shape, dt)
                    return v_

            def alloc_set(base, si):
                al = Alloc(base)
                B = {}
                nm = lambda n: "%s_%d" % (n, si)
                z0 = al.off
                B["zt"] = al(nm("zt"), [128, 4, 130])
                B["df"] = al(nm("df"), [128, 4, 128])
                z1 = al.off
                al2 = Alloc(z0)
                B["SUm"] = al2(nm("SUm"), [128, 4, 128], BF16)
                B["IUm"] = al2(nm("IUm"), [128, 4, 128], BF16)
                B["SLm"] = al2(nm("SLm"), [128, 2, 128], BF16)
                o_ = al2.off
                pp0 = (al2(nm("PPa0"), [128, 2, 128], BF16), al2(nm("PPb0"), [128, 2, 128], BF16),
                       rv(nm("PPab0"), o_, [128, 4, 128], BF16))
                assert al2.off <= z1
                B["zm"] = al(nm("zm"), [128, 4, 128])
                sig0 = al.off
                for k in ["sg", "icl", "sgate", "kk", "sq", "tq", "rn", "kkn", "kki", "tk", "k2", "rk", "bsum"]:
                    B[k] = al(nm(k), [128, 128])
                B["sig3"] = rv(nm("sig3"), sig0, [128, 384])
                for k in ["bT", "kT", "vb", "Wsb", "Usb"]:
                    B[k] = al(nm(k), [128, 128], BF16)
                B["tok"] = al(nm("tok"), [128, 3, 128], BF16)
                o_ = al.off
                pp1 = (al(nm("PPa1"), [128, 2, 128], BF16), al(nm("PPb1"), [128, 2, 128], BF16),
                       rv(nm("PPab1"), o_, [128, 4, 128], BF16))
                B["PP"] = [pp0, pp1]
                B["XT"] = [al(nm("XT%d" % i), [128, 2, 128], BF16) for i in range(2)]
                B["gx"], B["gi"], B["gm"] = B["sq"], B["tq"], B["rn"]
                B["yt"], B["cen"], B["yn"], B["o1"], B["slg"] = B["kk"], B["kkn"], B["kki"], B["tk"], B["k2"]
                B["sqc"], B["tv"], B["rsv"] = B["sq"], B["tq"], B["rn"]
                return B, al.off

            SETS = []
            B0, end0 = alloc_set(0, 0)
            B1, end1 = alloc_set(end0, 1)
            SETS = [B0, B1]
            print("region set size", end0, "two sets", end1, "of", RWORDS)
            for si, B in enumerate(SETS):
                B["cs"] = sb([128, 130], F32, "cs%d" % si)
                B["aTp"] = sb([128, 2, 128], BF16, "aTp%d" % si)
                B["rTp"] = sb([128, 2, 128], BF16, "rTp%d" % si)
                B["Hg"] = sb([128, 64], F32, "Hg%d" % si)
                G(lambda e, B=B: e.memset(B["cs"][:], 0.0), w=[B["cs"]])
                G(lambda e, B=B: e.memset(B["aTp"][:], 0.0), w=[B["aTp"]])
                G(lambda e, B=B: e.memset(B["rTp"][:], 0.0), w=[B["rTp"]])
            al = Alloc(0)
            gu = al("gu", [128, 512])
            gv = al("gv", [128, 512])
            eu = al("eu", [128, 512])
            sil = al("sil", [128, 512])
            vh = al("vh", [128, 512])
            y0 = al("y0", [128, 512])
            vn = al("vn", [128, 512], BF16)
            ysg = al("ysg", [128, 512], BF16)
            assert al.off <= end0, (al.off, end0)
            wflat = wkv_blk.t[:].rearrange("p a b -> p (a b)")
            Ee = Buf(wflat[:, 0:1024].rearrange("p (a b) -> p a b", b=256), "Ee")
            PT = Buf(wflat[:, 1024:2048].rearrange("p (a b) -> p a b", b=128), "PT")
            oT = Buf(wflat[:, 2048:3072].rearrange("p (a b) -> p a b", b=128), "oT")
            hs = Buf(wflat[:, 3072:4096], "hs")
            for v_ in (Ee, PT, oT, hs):
                v_.alias = [wkv_blk]
                wkv_blk.alias.append(v_)
            al = Alloc(end1)
            zw = al("zw", [128, 130])
            dfw = al("dfw", [128, 128])
            lw = al("lw", [128, 128], BF16)
            print("region end", al.off, "of", RWORDS)
            ot = stage
            ot = stage
            hnT = Buf(mnT.t[:, :, 0:128], "hnT")
            qT = Buf(mnT.t[:, :, 128:256], "qT")
            hnT.alias = [mnT]
            qT.alias = [mnT]
            mnT.alias = [hnT, qT]

            print("sbuf bytes remaining:", nc.sbuf_bytes_remaining)

            RW0 = 1536

            def mix_chunks(zbuf, nchunk, col0, dfbuf, zmbuf):
                V(lambda e: e.tensor_tensor(out=dfbuf[:], in0=zbuf[:, :, 0:128], in1=zbuf[:, :, 1:129], op=ALU.subtract),
                  r=[zbuf], w=[dfbuf])
                for j in range(nchunk):
                    V(lambda e, j=j: e.scalar_tensor_tensor(out=zmbuf[:, j, :], in0=dfbuf[:, j, :],
                                                            scalar=rwpp[:, MU + col0 + j * 4:MU + col0 + j * 4 + 1],
                                                            in1=zbuf[:, j, 1:129], op0=ALU.mult, op1=ALU.add),
                      r=[dfbuf, zbuf, rwpp], w=[zmbuf])

            pending_post = None
            zdone = {}
            PZ = {}

            def wait_flags(keys):
                while fw.hook is not None and not all(zdone.get(k, False) for k in keys):
                    fw.hook()

            def zblk(blk):
                p = npf()
                for c in range(8):
                    PE(lambda e, c=c, p=p: e.matmul(p[:], lhsT=xnT[:, c, :], rhs=w_in[:, c, blk * 512:(blk + 1) * 512],
                                                    start=(c == 0), stop=(c == 7)),
                       r=[xnT, w_in], w=[p], last=(c == 7))
                return p

            def pre_body(it, flags):
                r0 = it * 128
                xt = xt_bufs[it % 2]
                ob = ot[it % 2] if it > 0 else xt_bufs[1]
                gvv, guu = ob[:, 0:512], ob[:, 512:1024]
                vnn, ysgg = xs[:, 0:512], xs[:, 512:1024]
                fw.dma("sync", lambda e: e.dma_start(out=xt[:], in_=x_d[r0:r0 + 128, :]), xt, writes=[xt])
                rs = rstd_from_sumsq(xt, xt[:], D, RMS_EPS)
                V(lambda e: e.tensor_scalar(out=xs[:], in0=xt[:], scalar1=rs[:, 0:1], scalar2=None,
                                            op0=ALU.mult), r=[xt, rs], w=[xs])
                wait_flags([("z", k) for k in flags])
                transpose_to(xs, [xs[:, c * 128:(c + 1) * 128] for c in range(8)], xnT, xnT[:])
                zdone[("xnT",)] = True
                pv = zblk(1)
                pu = zblk(0)
                fw.atomic = True
                A(lambda e: e.activation(out=gvv, in_=pv[:], func=AF.Gelu), r=[pv], w=[ob])
                fw.atomic = False
                fw.atomic = True
                A(lambda e: e.activation(out=guu, in_=pu[:], func=AF.Gelu), r=[pu], w=[ob])
                fw.atomic = False
                V(lambda e: e.bn_stats(out=bn6[:], in_=gvv), r=[ob], w=[bn6])
                V(lambda e: e.bn_aggr(out=mv[:], in_=bn6[:]), r=[bn6], w=[mv])
                t1 = nst()
                A(lambda e: e.activation(out=t1[:, 0:1], in_=mv[:, 1:2], func=AF.Ln, bias=LN_EPS), r=[mv], w=[t1])
                A(lambda e: e.activation(out=t1[:, 1:2], in_=t1[:, 0:1], func=AF.Exp, scale=-0.5), r=[t1], w=[t1])
                V(lambda e: e.tensor_scalar(out=gvv, in0=gvv, scalar1=mv[:, 0:1], scalar2=t1[:, 1:2],
                                            op0=ALU.subtract, op1=ALU.mult), r=[ob, mv, t1], w=[ob])
                V(lambda e: e.tensor_tensor(out=gvv, in0=gvv, in1=lng_b[:], op=ALU.mult), r=[ob, lng_b], w=[ob])
                V(lambda e: e.tensor_tensor(out=vnn, in0=gvv, in1=lnb_b[:], op=ALU.add), r=[ob, lnb_b], w=[xs])
                psv = npf()
                for hh in range(4):
                    PE(lambda e, hh=hh: e.matmul(psv[:, hh * 128:(hh + 1) * 128], lhsT=wsTm[:, hh, :],
                                                 rhs=xs[:, hh * 128:(hh + 1) * 128], start=True, stop=True),
                       r=[wsTm, xs], w=[psv], last=(hh == 3))
                for hh in range(4):
                    V(lambda e, hh=hh: e.scalar_tensor_tensor(out=ob[:, 512 + hh * 128:512 + (hh + 1) * 128],
                                                              in0=psv[:, hh * 128:(hh + 1) * 128],
                                                              scalar=bsT[:, hh:hh + 1],
                                                              in1=ob[:, 512 + hh * 128:512 + (hh + 1) * 128],
                                                              op0=ALU.add, op1=ALU.mult), r=[psv, bsT, ob], w=[ob])
                rs2 = rstd_from_sumsq(ob, guu, 512, RMS_EPS)
                pg = zblk(2)
                sigmoid3(ob, gvv, pg, pg[:])
                V(lambda e: e.tensor_tensor(out=gvv, in0=gvv, in1=pg[:], op=ALU.mult), r=[ob, pg], w=[ob])
                V(lambda e: e.scalar_tensor_tensor(out=ysgg, in0=guu, scalar=rs2[:, 0:1], in1=gvv,
                                                   op0=ALU.mult, op1=ALU.mult), r=[ob, rs2], w=[xs])
                if it == 0:
                    wait_flags([("wr",)])
                pw = npf()
                for c in range(8):
                    PE(lambda e, c=c: e.matmul(pw[:, 0:128], lhsT=w_in[:, c, RW0 + 2048:RW0 + 2176], rhs=xnT[:, c, :],
                                               start=(c == 0), stop=(c == 7)), r=[w_in_r, xnT], w=[pw], last=(c == 7))
                A(lambda e: e.copy(out=zw[:, 0:1], in_=carry[:, 16:17]), r=[carry], w=[zw])
                A(lambda e: e.copy(out=zw[:, 1:129], in_=pw[:, 0:128]), r=[pw], w=[zw])
                A(lambda e: e.copy(out=carry[:, 16:17], in_=zw[:, 128:129]), r=[zw], w=[carry])
                V(lambda e: e.tensor_tensor(out=dfw[:], in0=zw[:, 0:128], in1=zw[:, 1:129], op=ALU.subtract),
                  r=[zw], w=[dfw])
                V(lambda e: e.scalar_tensor_tensor(out=dfw[:], in0=dfw[:], scalar=rwpp[:, MU + 16:MU + 17],
                                                   in1=zw[:, 1:129], op0=ALU.mult, op1=ALU.add),
                  r=[dfw, zw, rwpp], w=[dfw])
                wait_flags([("lo", k) for k in flags])
                A(lambda e: e.copy(out=lw[64:128, :], in_=dfw[64:128, :]), r=[dfw], w=[lw])
                A(lambda e: e.activation(out=dfw[0:64, :], in_=dfw[0:64, :], func=AF.Exp, scale=-2.0), r=[dfw], w=[dfw])
                A(lambda e: e.activation(out=dfw[0:64, :], in_=dfw[0:64, :], func=AF.Ln, bias=1.0), r=[dfw], w=[dfw])
                A(lambda e: e.activation(out=dfw[0:64, :], in_=dfw[0:64, :], func=AF.Exp, scale=-1.0), r=[dfw], w=[dfw])
                V(lambda e: e.tensor_scalar(out=lw[0:64, :], in0=dfw[0:64, :], scalar1=2.0, scalar2=-1.0,
                                            op0=ALU.mult, op1=ALU.add), r=[dfw], w=[lw])

            def finish_pre():
                transpose_to(xs, [xs[:, 512 + c * 128:512 + (c + 1) * 128] for c in range(4)], yT, yT[:, 0:4, :])

            def pre0():
                pre_body(0, [])
                finish_pre()
            fw.run_threads([in_pool("post", pre0), load_w_in_r])
            for it in range(ntile):
                r0 = it * 128
                xt = xt_bufs[it % 2]
                zdone.clear()

                def emit_z(c):
                    pz = npf()
                    for j in range(4):
                        col = RW0 + j * 512 + c * 128
                        for cc in range(8):
                            PE(lambda e, j=j, cc=cc, col=col: e.matmul(pz[:, j * 128:(j + 1) * 128],
                                                                        lhsT=w_in[:, cc, col:col + 128], rhs=xnT[:, cc, :],
                                                                        start=(cc == 0), stop=(cc == 7)),
                               r=[w_in_r, xnT], w=[pz], last=(j == 3 and cc == 7))
                    return pz

                def pair_body(c, B, si=0, nxt=None):
                    zt, df, zm, sg, icl, sgate, kk, sq, tq, rn, kkn, kki, tk, k2, rk, bsum, gx, gi, gm, bT, kT, vb, tok, SUm, IUm, SLm, PP, XT, Wsb, Usb, yt, cen, yn, o1, slg, sqc, tv, rsv, cs, aTp, rTp, Hg, sig3 = [B[k] for k in ['zt', 'df', 'zm', 'sg', 'icl', 'sgate', 'kk', 'sq', 'tq', 'rn', 'kkn', 'kki', 'tk', 'k2', 'rk', 'bsum', 'gx', 'gi', 'gm', 'bT', 'kT', 'vb', 'tok', 'SUm', 'IUm', 'SLm', 'PP', 'XT', 'Wsb', 'Usb', 'yt', 'cen', 'yn', 'o1', 'slg', 'sqc', 'tv', 'rsv', 'cs', 'aTp', 'rTp', 'Hg', 'sig3']]
                    pz = PZ.pop(si, None)
                    if pz is None:
                        pz = emit_z(c)
                    A(lambda e, c=c: e.copy(out=zt[:, :, 0:1],
                                                   in_=carry[:, 0:16].rearrange("p (j c) -> p j c", c=4)[:, :, c:c + 1]),
                      r=[carry], w=[zt])
                    A(lambda e: e.copy(out=zt[:, :, 1:129], in_=pz[:].rearrange("p (a b) -> p a b", b=128)),
                      r=[pz], w=[zt])
                    A(lambda e, c=c: e.copy(out=carry[:, 0:16].rearrange("p (j c) -> p j c", c=4)[:, :, c:c + 1],
                                                   in_=zt[:, :, 128:129]), r=[zt], w=[carry])
                    zdone[("z", c)] = True
                    mix_chunks(zt, 4, c, df, zm)
                    r_, k_, v_, g_ = zm[:, 0, :], zm[:, 1, :], zm[:, 2, :], zm[:, 3, :]
                    pl = npf()
                    PE(lambda e, c=c: e.matmul(pl[:, 0:128], lhsT=w2b[:, c * 128:(c + 1) * 128], rhs=lw[:],
                                               start=True, stop=True), r=[w2b, lw], w=[pl], last=False)
                    PE(lambda e, c=c: e.matmul(pl[:, 128:256], lhsT=a2b[:, c * 128:(c + 1) * 128], rhs=lw[:],
                                               start=True, stop=True), r=[a2b, lw], w=[pl], last=True)
                    zdone[("lo", c)] = True
                    A(lambda e, c=c: e.activation(out=sg[:], in_=pl[:, 0:128], func=AF.Exp, scale=-1.0,
                                                  bias=nrw[:, c:c + 1]), r=[pl, nrw], w=[sg])
                    A(lambda e, c=c: e.activation(out=icl[:], in_=pl[:, 128:256], func=AF.Exp, scale=-1.0,
                                                  bias=nrw[:, 4 + c:5 + c]), r=[pl, nrw], w=[icl])
                    A(lambda e: e.activation(out=sgate[:], in_=g_, func=AF.Exp, scale=-1.0), r=[zm], w=[sgate])
                    A(lambda e: e.activation(out=sig3[:], in_=sig3[:], func=AF.Ln, bias=1.0), r=[sig3], w=[sig3])
                    A(lambda e: e.activation(out=sig3[:], in_=sig3[:], func=AF.Exp, scale=-1.0), r=[sig3], w=[sig3])
                    V(lambda e: e.tensor_tensor_scan(out=cs[:, 1:129], data0=ones512[:, 0:128], data1=sg[:], initial=0.0,
                                                     op0=ALU.mult, op1=ALU.add), r=[ones512, sg], w=[cs])
                    V(lambda e, c=c: e.tensor_scalar(out=kk[:], in0=k_, scalar1=rwpp[:, KK + c:KK + c + 1], scalar2=None,
                                                     op0=ALU.mult), r=[zm, rwpp], w=[kk])
                    A(lambda e, c=c: e.activation(out=sq[:], in_=k_, func=AF.Square, scale=rwpp[:, KK + c:KK + c + 1]), r=[zm, rwpp], w=[sq])
                    pn = npf()
                    PE(lambda e: e.matmul(pn[:, 0:128], lhsT=onesbd[:], rhs=sq[:], start=True, stop=True),
                       r=[onesbd, sq], w=[pn])
                    V(lambda e: e.tensor_scalar(out=tq[:], in0=pn[:, 0:128], scalar1=1e-24, scalar2=None, op0=ALU.max),
                      r=[pn], w=[tq])
                    A(lambda e: e.activation(out=tq[:], in_=tq[:], func=AF.Ln), r=[tq], w=[tq])
                    A(lambda e: e.activation(out=rn[:], in_=tq[:], func=AF.Exp, scale=-0.5), r=[tq], w=[rn])
                    V(lambda e: e.tensor_tensor(out=kkn[:], in0=kk[:], in1=rn[:], op=ALU.mult), r=[kk, rn], w=[kkn])
                    V(lambda e: e.tensor_tensor(out=kki[:], in0=kkn[:], in1=icl[:], op=ALU.mult), r=[kkn, icl], w=[kki])
                    V(lambda e, c=c: e.tensor_scalar(out=tk[:], in0=icl[:], scalar1=-1.0, scalar2=rwpp[:, KA + c:KA + c + 1],
                                                     op0=ALU.add, op1=ALU.mult), r=[icl, rwpp], w=[tk])
                    V(lambda e: e.scalar_tensor_tensor(out=k2[:], in0=tk[:], scalar=1.0, in1=k_, op0=ALU.add, op1=ALU.mult),
                      r=[tk, zm], w=[k2])
                    V(lambda e, c=c: e.scalar_tensor_tensor(out=rk[:], in0=r_, scalar=rwpp[:, RK + c:RK + c + 1], in1=k2[:],
                                                            op0=ALU.mult, op1=ALU.mult), r=[zm, rwpp, k2], w=[rk])
                    pbs = npf()
                    PE(lambda e: e.matmul(pbs[:, 0:128], lhsT=onesbd[:], rhs=rk[:], start=True, stop=True),
                       r=[onesbd, rk], w=[pbs])
                    A(lambda e: e.copy(out=bsum[:], in_=pbs[:, 0:128]), r=[pbs], w=[bsum])
                    A(lambda e: e.activation(out=gx[:], in_=cs[:, 0:128], func=AF.Exp, scale=-C0), r=[cs], w=[gx])
                    A(lambda e: e.activation(out=gi[:], in_=cs[:, 1:129], func=AF.Exp, scale=C0), r=[cs], w=[gi])
                    A(lambda e: e.activation(out=gm[:], in_=cs[:, 1:129], func=AF.Exp, scale=-C0), r=[cs], w=[gm])
                    for hh in range(2):
                        P = slice(hh * 64, hh * 64 + 64)
                        V(lambda e, hh=hh, P=P: e.scalar_tensor_tensor(out=aTp[P, hh, :], in0=kkn[P, :], scalar=-1.0,
                                                                       in1=gx[P, :], op0=ALU.mult, op1=ALU.mult),
                          r=[kkn, gx], w=[aTp])
                    V(lambda e: e.tensor_tensor(out=bT[:], in0=kki[:], in1=gi[:], op=ALU.mult), r=[kki, gi], w=[bT])
                    V(lambda e: e.tensor_tensor(out=kT[:], in0=k2[:], in1=gi[:], op=ALU.mult), r=[k2, gi], w=[kT])
                    for hh in range(2):
                        P = slice(hh * 64, hh * 64 + 64)
                        V(lambda e, hh=hh, P=P: e.tensor_tensor(out=rTp[P, hh, :], in0=zm[P, 0, :], in1=gm[P, :],
                                                                op=ALU.mult), r=[zm, gm], w=[rTp])
                    A(lambda e: e.copy(out=vb[:], in_=v_), r=[zm], w=[vb])
                    ptk = npb()
                    for k, (bb, ap) in enumerate([(bT, bT[:]), (kT, kT[:]), (vb, vb[:])]):
                        PE(lambda e, k=k, ap=ap: e.transpose(out=ptk[:, k * 128:(k + 1) * 128], in_=ap, identity=identb[:]),
                           r=[bb, identb], w=[ptk], last=(k == 2))
                    A(lambda e: e.copy(out=tok[:], in_=ptk[:, 0:384].rearrange("p (a b) -> p a b", b=128)),
                      r=[ptk], w=[tok])
                    pSU = npf()
                    for hh in range(2):
                        PE(lambda e, hh=hh: e.matmul(pSU[:, hh * 128:(hh + 1) * 128], lhsT=bT[:], rhs=aTp[:, hh, :],
                                                     start=True, stop=True), r=[bT, aTp], w=[pSU], last=False)
                        PE(lambda e, hh=hh: e.matmul(pSU[:, (2 + hh) * 128:(3 + hh) * 128], lhsT=kT[:], rhs=aTp[:, hh, :],
                                                     start=True, stop=True), r=[kT, aTp], w=[pSU], last=(hh == 1))
                    V(lambda e: e.tensor_tensor(out=SUm[:], in0=pSU[:].rearrange("p (a b) -> p a b", b=128), in1=mSU[:],
                                                op=ALU.mult), r=[pSU, mSU], w=[SUm])
                    pSL = npf()
                    for hh in range(2):
                        PE(lambda e, hh=hh: e.matmul(pSL[:, hh * 128:(hh + 1) * 128], lhsT=aTp[:, hh, :], rhs=bT[:],
                                                     start=True, stop=True), r=[aTp, bT], w=[pSL], last=(hh == 1))
                    V(lambda e: e.tensor_tensor(out=SLm[:], in0=pSL[:, 0:256].rearrange("p (a b) -> p a b", b=128),
                                                in1=mSL[:], op=ALU.mult), r=[pSL, mSL], w=[SLm])
                    pIU = npf()
                    for hh in range(2):
                        PE(lambda e, hh=hh: e.matmul(pIU[:, hh * 128:(hh + 1) * 128], lhsT=bT[:], rhs=rTp[:, hh, :],
                                                     start=True, stop=True), r=[bT, rTp], w=[pIU], last=False)
                        PE(lambda e, hh=hh: e.matmul(pIU[:, (2 + hh) * 128:(3 + hh) * 128], lhsT=kT[:], rhs=rTp[:, hh, :],
                                                     start=True, stop=True), r=[kT, rTp], w=[pIU], last=(hh == 1))
                    V(lambda e: e.tensor_tensor(out=IUm[:], in0=pIU[:].rearrange("p (a b) -> p a b", b=128), in1=mIU[:],
                                                op=ALU.mult), r=[pIU, mIU], w=[IUm])
                    V(lambda e: e.tensor_tensor(out=XT[1][:], in0=SUm[:, 0:2, :], in1=ident2[:], op=ALU.add),
                      r=[SUm, ident2], w=[XT[1]])
                    curP = (SLm, lambda hh: SLm[:, hh, :])
                    curPT = (SUm, lambda hh: SUm[:, hh, :])

                    def x_update(k, pk):
                        xo = XT[k % 2]
                        xn_ = XT[(k + 1) % 2]
                        px = npf()
                        for hh in range(2):
                            PE(lambda e, hh=hh: e.matmul(px[:, hh * 128:(hh + 1) * 128], lhsT=pk[1](hh),
                                                         rhs=xo[:, hh, :], start=True, stop=True),
                               r=[pk[0], xo], w=[px], last=(hh == 1))
                        V(lambda e: e.tensor_tensor(out=xn_[:], in0=px[:, 0:256].rearrange(
                            "p (a b) -> p a b", b=128), in1=xo[:], op=ALU.add), r=[px, xo], w=[xn_])

                    for lev in range(1, 7):
                        pq = npf()
                        for hh in range(2):
                            PE(lambda e, hh=hh, cp=curP, cpt=curPT: e.matmul(pq[:, hh * 128:(hh + 1) * 128],
                                                                             lhsT=cpt[1](hh), rhs=cp[1](hh),
                                                                             start=True, stop=True),
                               r=[curP[0], curPT[0]], w=[pq], last=(lev == 6 and hh == 1))
                        if lev < 6:
                            for hh in range(2):
                                PE(lambda e, hh=hh, cp=curP, cpt=curPT: e.matmul(pq[:, (2 + hh) * 128:(3 + hh) * 128],
                                                                                 lhsT=cp[1](hh), rhs=cpt[1](hh),
                                                                                 start=True, stop=True),
                                   r=[curP[0], curPT[0]], w=[pq], last=(hh == 1))
                        npa, npb_, npab = PP[lev % 2]
                        ev = A if lev != 3 else V
                        if lev < 6:
                            if ev is A:
                                A(lambda e, npab=npab: e.copy(out=npab[:], in_=pq[:].rearrange("p (a b) -> p a b", b=128)),
                                  r=[pq], w=[npab])
                            else:
                                V(lambda e, npab=npab: e.tensor_copy(out=npab[:],
                                                                     in_=pq[:].rearrange("p (a b) -> p a b", b=128)),
                                  r=[pq], w=[npab])
                        else:
                            A(lambda e, npa=npa: e.copy(out=npa[:], in_=pq[:, 0:256].rearrange("p (a b) -> p a b", b=128)),
                              r=[pq], w=[npa])
                        prevP = curP
                        curP = (npa, lambda hh, npa=npa: npa[:, hh, :])
                        curPT = (npb_, lambda hh, npb_=npb_: npb_[:, hh, :])
                        if lev >= 2:
                            x_update(lev - 1, prevP)
                    x_update(6, curP)
                    XTf = XT[1]

                    gC = gm[:, 127:128]
                    V(lambda e, c=c, gC=gC: e.tensor_scalar(out=Hg[:], in0=Hs[c][:], scalar1=gC, scalar2=None, op0=ALU.mult),
                      r=[Hs[c], gm], w=[Hg])
                    pW = npf()
                    for hh in range(2):
                        PE(lambda e, hh=hh, c=c: e.matmul(pW[:, hh * 64:(hh + 1) * 64], lhsT=aTp[:, hh, :], rhs=Hbd[c][:, hh, :],
                                                          start=True, stop=False), r=[aTp, Hbd[c]], w=[pW], last=False)
                        PE(lambda e, hh=hh: e.matmul(pW[:, hh * 64:(hh + 1) * 64], lhsT=SUm[:, 2 + hh, :],
                                                     rhs=tok[:, 2, hh * 64:(hh + 1) * 64], start=False, stop=True),
                           r=[SUm, tok], w=[pW], last=(hh == 1))
                    A(lambda e: e.copy(out=Wsb[:], in_=pW[:, 0:128]), r=[pW], w=[Wsb])
                    pU = npf()
                    for hh in range(2):
                        PE(lambda e, hh=hh, XTf=XTf: e.matmul(pU[:, hh * 64:(hh + 1) * 64], lhsT=XTf[:, hh, :],
                                                              rhs=Wsb[:, hh * 64:(hh + 1) * 64], start=True, stop=True),
                           r=[XTf, Wsb], w=[pU], last=(hh == 1))
                    V(lambda e: e.tensor_copy(out=Usb[:], in_=pU[:, 0:128]), r=[pU], w=[Usb])
                    pY = npf()
                    for hh in range(2):
                        P = slice(hh * 64, hh * 64 + 64)
                        PE(lambda e, hh=hh, P=P, c=c: e.matmul(pY[P, 0:128], lhsT=Hbd[c][:, hh, :], rhs=rTp[:, hh, :],
                                                               start=True, stop=False), r=[Hbd[c], rTp], w=[pY], last=False)
                        PE(lambda e, hh=hh, P=P: e.matmul(pY[P, 0:128], lhsT=Usb[:, hh * 64:(hh + 1) * 64],
                                                          rhs=IUm[:, hh, :], start=False, stop=False),
                           r=[Usb, IUm], w=[pY], last=False)
                        PE(lambda e, hh=hh, P=P: e.matmul(pY[P, 0:128], lhsT=tok[:, 2, hh * 64:(hh + 1) * 64],
                                                          rhs=IUm[:, 2 + hh, :], start=False, stop=True),
                           r=[tok, IUm], w=[pY], last=(hh == 1))
                    A(lambda e: e.copy(out=yt[:], in_=pY[:, 0:128]), r=[pY], w=[yt])
                    pH = npf()
                    for hh in range(2):
                        P = slice(hh * 64, hh * 64 + 64)
                        PE(lambda e, hh=hh, P=P: e.matmul(pH[P, 0:64], lhsT=tok[:, 0, hh * 64:(hh + 1) * 64],
                                                          rhs=Usb[:, hh * 64:(hh + 1) * 64], start=True, stop=False),
                           r=[tok, Usb], w=[pH], last=False)
                        PE(lambda e, hh=hh, P=P: e.matmul(pH[P, 0:64], lhsT=tok[:, 1, hh * 64:(hh + 1) * 64],
                                                          rhs=tok[:, 2, hh * 64:(hh + 1) * 64], start=False, stop=True),
                           r=[tok], w=[pH], last=(hh == 1))
                    V(lambda e, c=c, gC=gC: e.scalar_tensor_tensor(out=Hs[c][:], in0=pH[:, 0:64], scalar=gC, in1=Hg[:],
                                                                   op0=ALU.mult, op1=ALU.add), r=[pH, gm, Hg], w=[Hs[c]])
                    A(lambda e, c=c: e.copy(out=Hbd[c][0:64, 0, :], in_=Hs[c][0:64, :]), r=[Hs[c]], w=[Hbd[c]])
                    A(lambda e, c=c: e.copy(out=Hbd[c][64:128, 1, :], in_=Hs[c][64:128, :]), r=[Hs[c]], w=[Hbd[c]])

                    pm = npf()
                    PE(lambda e: e.matmul(pm[:, 0:128], lhsT=onesbd64[:], rhs=yt[:], start=True, stop=True),
                       r=[onesbd64, yt], w=[pm])
                    V(lambda e: e.tensor_tensor(out=cen[:], in0=yt[:], in1=pm[:, 0:128], op=ALU.subtract), r=[yt, pm], w=[cen])
                    A(lambda e: e.activation(out=sqc[:], in_=cen[:], func=AF.Square), r=[cen], w=[sqc])
                    pvv = npf()
                    PE(lambda e: e.matmul(pvv[:, 0:128], lhsT=onesbd64[:], rhs=sqc[:], start=True, stop=True),
                       r=[onesbd64, sqc], w=[pvv])
                    A(lambda e: e.activation(out=tv[:], in_=pvv[:, 0:128], func=AF.Ln, bias=GN_EPS), r=[pvv], w=[tv])
                    A(lambda e: e.activation(out=rsv[:], in_=tv[:], func=AF.Exp, scale=-0.5), r=[tv], w=[rsv])
                    V(lambda e: e.tensor_tensor(out=yn[:], in0=cen[:], in1=rsv[:], op=ALU.mult), r=[cen, rsv], w=[yn])
                    A(lambda e, c=c: e.activation(out=yn[:], in_=yn[:], func=AF.Identity, scale=rwpp[:, GG + c:GG + c + 1],
                                                  bias=rwpp[:, GB + c:GB + c + 1]), r=[yn, rwpp], w=[yn])
                    V(lambda e: e.tensor_tensor(out=o1[:], in0=bsum[:], in1=v_, op=ALU.mult), r=[bsum, zm], w=[o1])
                    V(lambda e: e.tensor_tensor(out=o1[:], in0=o1[:], in1=yn[:], op=ALU.add), r=[o1, yn], w=[o1])
                    V(lambda e: e.tensor_tensor(out=slg[:], in0=g_, in1=sgate[:], op=ALU.mult), r=[zm, sgate], w=[slg])
                    if it > 0:
                        wait_flags([("D", it - 1)])
                    V(lambda e, c=c: e.tensor_tensor(out=yT[:, 4 + c, :], in0=o1[:], in1=slg[:], op=ALU.mult),
                      r=[o1, slg], w=[yT])
                    if nxt is not None:
                        c_n, need_xnT = nxt
                        if need_xnT:
                            wait_flags([("xnT",)])
                        PZ[si] = emit_z(c_n)
                        if not need_xnT:
                            zdone[("z", c_n)] = True

                nx2 = (lambda cn: (cn, True)) if it + 1 < ntile else (lambda cn: None)

                def t_a():
                    pair_body(0, SETS[0], 0, (2, False))
                    pair_body(2, SETS[0], 0, nx2(0))

                def t_b():
                    pair_body(1, SETS[1], 1, (3, False))
                    pair_body(3, SETS[1], 1, nx2(1))

                def t_c(pp=pending_post):
                    if pp is not None:
                        pp()
                    elif it == 0:
                        mem_phase()
                    if it == 0:
                        load_w_out()
                    if it + 1 < ntile:
                        pre_body(it + 1, [2, 3])
                pending_post = None
                fw.run_threads([in_pool("pA", t_a), in_pool("pB", t_b), in_pool("post", t_c)])
                def d_body(xt, it, last):
                    for blk in range(2):
                        p = npf()
                        for c in range(8):
                            PE(lambda e, c=c, p=p, blk=blk: e.matmul(p[:], lhsT=yT[:, c, :],
                                                                      rhs=w_out[:, c, blk * 512:(blk + 1) * 512],
                                                                      start=(c == 0), stop=(c == 7)),
                               r=[yT, w_out], w=[p], last=(c == 7))
                        V(lambda e, p=p, blk=blk: e.tensor_tensor(out=xt[:, blk * 512:(blk + 1) * 512],
                                                                  in0=xt[:, blk * 512:(blk + 1) * 512], in1=p[:],
                                                                  op=ALU.add), r=[xt, p], w=[xt])
                    zdone[("D", it)] = True
                    if not last:
                        finish_pre()
                h = xt
                ck('D')
                def post_body(h, r0, it):
                    rs3 = rstd_from_sumsq(h, h[:], D, RMS_EPS)
                    V(lambda e, rs3=rs3: e.tensor_scalar(out=hs[:], in0=h[:], scalar1=rs3[:, 0:1], scalar2=None, op0=ALU.mult),
                      r=[h, rs3], w=[hs])
                    transpose_to(hs, [hs[:, c * 128:(c + 1) * 128] for c in range(8)], hnT, hnT[:])
                    for half in range(2):
                        p = npf()
                        for jj in range(4):
                            j = half * 4 + jj
                            for c in range(8):
                                PE(lambda e, j=j, jj=jj, c=c, p=p: e.matmul(p[:, jj * 128:(jj + 1) * 128],
                                                                             lhsT=w_q[:, c, j * 128:(j + 1) * 128],
                                                                             rhs=hnT[:, c, :], start=(c == 0), stop=(c == 7)),
                                   r=[w_q, hnT], w=[p], last=(jj == 3 and c == 7))
                        A(lambda e, p=p, half=half: e.copy(out=qT[:, half * 4:(half + 1) * 4, :],
                                                           in_=p[:].rearrange("p (a b) -> p a b", b=128)), r=[p], w=[qT])
                    for half in range(2):
                        p = npf()
                        for hh in range(2):
                            h4 = half * 2 + hh
                            for jj in range(2):
                                PE(lambda e, h4=h4, hh=hh, jj=jj, p=p: e.matmul(p[:, hh * 256:(hh + 1) * 256],
                                                                                 lhsT=qT[:, 2 * h4 + jj, :],
                                                                                 rhs=KT[:, 2 * h4 + jj, :],
                                                                                 start=(jj == 0), stop=(jj == 1)),
                                   r=[qT, KT], w=[p], last=(hh == 1 and jj == 1))
                        V(lambda e, p=p, half=half: e.tensor_reduce(out=mx[:, half * 2:half * 2 + 2],
                                                                    in_=p[:].rearrange("p (a b) -> p a b", b=256),
                                                                    axis=AX.X, op=ALU.max), r=[p], w=[mx])
                        V(lambda e, half=half: e.tensor_scalar(out=mx[:, 4 + half * 2:6 + half * 2],
                                                               in0=mx[:, half * 2:half * 2 + 2], scalar1=-1.0, scalar2=None,
                                                               op0=ALU.mult), r=[mx], w=[mx])
                        for hh in range(2):
                            h4 = half * 2 + hh
                            A(lambda e, p=p, hh=hh, h4=h4: e.activation(out=Ee[:, h4, :], in_=p[:, hh * 256:(hh + 1) * 256],
                                                                        func=AF.Exp, bias=mx[:, 4 + h4:5 + h4],
                                                                        accum_out=rsum[:, h4:h4 + 1]),
                              r=[p, mx], w=[Ee, rsum])
                    V(lambda e: e.reciprocal(out=rsum[:, 4:8], in_=rsum[:, 0:4]), r=[rsum], w=[rsum])
                    for h4 in range(4):
                        V(lambda e, h4=h4: e.tensor_scalar(out=Ee[:, h4, :], in0=Ee[:, h4, :], scalar1=rsum[:, 4 + h4:5 + h4],
                                                           scalar2=None, op0=ALU.mult), r=[Ee, rsum], w=[Ee])
                    transpose_to(Ee, [Ee[:, k // 2, (k % 2) * 128:(k % 2 + 1) * 128] for k in range(8)], PT, PT[:])
                    for half in range(2):
                        p = npf()
                        for jj in range(4):
                            j = half * 4 + jj
                            h4, dd = j // 2, j % 2
                            for mc in range(2):
                                PE(lambda e, jj=jj, h4=h4, dd=dd, mc=mc, p=p: e.matmul(
                                    p[:, jj * 128:(jj + 1) * 128],
                                    lhsT=Vm[:, mc, h4 * 256 + dd * 128:h4 * 256 + (dd + 1) * 128],
                                    rhs=PT[:, h4 * 2 + mc, :], start=(mc == 0), stop=(mc == 1)),
                                   r=[Vm, PT], w=[p], last=(jj == 3 and mc == 1))
                        A(lambda e, p=p, half=half: e.copy(out=oT[:, half * 4:(half + 1) * 4, :],
                                                           in_=p[:].rearrange("p (a b) -> p a b", b=128)), r=[p], w=[oT])
                    for blk in range(2):
                        p = npf()
                        for c in range(8):
                            PE(lambda e, c=c, p=p, blk=blk: e.matmul(p[:], lhsT=oT[:, c, :],
                                                                      rhs=w_o[:, c, blk * 512:(blk + 1) * 512],
                                                                      start=(c == 0), stop=(c == 7)),
                               r=[oT, w_o], w=[p], last=(c == 7))
                        V(lambda e, p=p, blk=blk: e.tensor_tensor(out=h[:, blk * 512:(blk + 1) * 512],
                                                                  in0=h[:, blk * 512:(blk + 1) * 512], in1=p[:], op=ALU.add),
                          r=[h, p], w=[h])
                    rs4 = rstd_from_sumsq(h, h[:], D, RMS_EPS)
                    ob = ot[it % 2]
                    V(lambda e, rs4=rs4, ob=ob: e.scalar_tensor_tensor(out=ob[:], in0=h[:], scalar=rs4[:, 0:1], in1=lnf_b[:],
                                                                       op0=ALU.mult, op1=ALU.mult), r=[h, rs4, lnf_b], w=[ob])
                    fw.dma("sync", lambda e, ob=ob, r0=r0: e.dma_start(out=out_d[r0:r0 + 128, :], in_=ob[:]), ob, reads=[ob])
                    fw.mark_out(ob)
                if it == 0:
                    def post0(h=h, r0=r0, post_body=post_body, d_body=d_body):
                        d_body(h, 0, ntile == 1)
                        load_w_q()
                        load_w_o()
                        post_body(h, r0, 0)
                    pending_post = post0
                else:
                    def postn(h=h, r0=r0, it=it, post_body=post_body, d_body=d_body):
                        d_body(h, it, it + 1 >= ntile)
                        post_body(h, r0, it)
                    pending_post = postn

            if pending_post is not None:
                tls.pool = "post"
                pending_post()

        try:
            record()
        except StopBuild:
            print('stopped at', stop)
        fw.build(st)
        ninst = {k: len(v.prog) for k, v in fw.E.items()}
        print("program entries:", ninst)
    return nc


_NC = None


def kernel(x, mem, ln_mix_g, w_in, sgu_ln_g, sgu_ln_b, sgu_ws, sgu_bs, sgu_out_g,
           rw_mu, rw_w0, rw_w2, rw_a0, rw_a2, rw_k_k, rw_k_a, rw_r_k, rw_gn_g, rw_gn_b,
           w_out, ln_x_g, ln_mem_g, w_q, w_kv, w_o, ln_f_g):
    global _NC
    f = lambda a: np.ascontiguousarray(np.asarray(a, dtype=np.float32))
    pp = lambda v, n: f(np.asarray(v, dtype=np.float32).reshape(n, 128).T)
    gpp = np.zeros((128, 32), np.float32)
    gpp[:, 0:8] = pp(ln_mix_g[0], 8)
    gpp[:, 8:16] = pp(ln_x_g[0], 8)
    gpp[:, 16:24] = pp(ln_mem_g[0], 8)
    gpp[:, 24:28] = pp(sgu_out_g[0], 4)
    rwpp = np.zeros((128, 48), np.float32)
    rwpp[:, 0:17] = pp(rw_mu[0], 17)
    rwpp[:, 17:21] = pp(rw_w0[0], 4)
    rwpp[:, 21:25] = pp(rw_a0[0], 4)
    rwpp[:, 25:29] = pp(rw_k_k[0], 4)
    rwpp[:, 29:33] = pp(rw_k_a[0], 4)
    rwpp[:, 33:37] = pp(np.asarray(rw_r_k[0]).reshape(-1), 4)
    rwpp[:, 37:41] = pp(rw_gn_g[0], 4)
    rwpp[:, 41:45] = pp(rw_gn_b[0], 4)
    shared = {
        "w_in": f(w_in[0]), "w_out": f(w_out[0]), "w_q": f(w_q[0]), "w_kv": f(w_kv[0]), "w_o": f(w_o[0]),
        "gpp": gpp, "lnf": f(np.asarray(ln_f_g).reshape(1, D)), "lng": f(np.asarray(sgu_ln_g[0]).reshape(1, 512)),
        "lnb": f(np.asarray(sgu_ln_b[0]).reshape(1, 512)),
        "wsT": f(np.transpose(np.asarray(sgu_ws[0]), (0, 2, 1))),
        "bsT": f(np.asarray(sgu_bs[0]).T), "rwpp": rwpp, "w2": f(rw_w2[0]), "a2": f(rw_a2[0]),
    }
    if _NC is None:
        _NC = build_program()
    x = np.asarray(x, dtype=np.float32)
    mem = np.asarray(mem, dtype=np.float32)
    in_maps = []
    for b in range(8):
        m = dict(shared)
        m["x"] = f(x[b])
        m["mem"] = f(mem[b])
        in_maps.append(m)
    res = run_bass_kernel_spmd(_NC, in_maps, core_ids=list(range(8)))
    out = np.stack([np.asarray(r["out"], dtype=np.float32) for r in res.results], axis=0)
    return out
```
